# Optimizing a Trainium2 kernel written in Bass

```python
import math
import jax, jax.numpy as jnp
from jax import lax
import numpy as np

D_MODEL = 1024
BATCH = 16
SEQ = 4096
DEPTH = 1

MIX_WIDTH = D_MODEL
GLA_WIDTH = MIX_WIDTH // 2
S5_WIDTH = MIX_WIDTH - GLA_WIDTH
GLA_HEADS = 4
GLA_KEY_DIM = GLA_WIDTH // 2
GLA_DK = GLA_KEY_DIM // GLA_HEADS
GLA_DV = GLA_WIDTH // GLA_HEADS
GLA_GATE_RANK = 16
GLA_GATE_NORMALIZER = 16.0
GLA_CHUNK = 64
S5_GROUP = 16
S5_GROUPS = S5_WIDTH // S5_GROUP
S5_STATE = 64
S5_DT_MIN = 1e-3
S5_DT_MAX = 1e-1
D_FF = 4 * D_MODEL
EPS = 1e-6
IN_COLS = 2 * GLA_KEY_DIM + 2 * GLA_WIDTH + GLA_GATE_RANK + S5_WIDTH
IN_SPLITS = (
    GLA_KEY_DIM,
    2 * GLA_KEY_DIM,
    2 * GLA_KEY_DIM + GLA_WIDTH,
    2 * GLA_KEY_DIM + GLA_WIDTH + GLA_GATE_RANK,
    2 * GLA_KEY_DIM + 2 * GLA_WIDTH + GLA_GATE_RANK,
)

kernel_name = "hymba_style_gla_s5_hybrid"


def rmsnorm(x, w):
    xf = x.astype(jnp.float32)
    y = xf * lax.rsqrt(jnp.mean(xf * xf, axis=-1, keepdims=True) + EPS)
    return (y * w.astype(jnp.float32)).astype(x.dtype)


def gla_mixer(q, k, v, gk_lr, g, w_gk_up, b_gk, norm_w):
    f32 = jnp.float32
    bsz, seq, _ = q.shape
    n_chunks = seq // GLA_CHUNK

    def heads(t, d):
        t = t.astype(f32).reshape(bsz, n_chunks, GLA_CHUNK, GLA_HEADS, d)
        return t.transpose(0, 3, 1, 2, 4)

    gk = gk_lr.astype(f32) @ w_gk_up.astype(f32) + b_gk.astype(f32)
    log_a = jax.nn.log_sigmoid(gk) / GLA_GATE_NORMALIZER

    qh = heads(q, GLA_DK) * (GLA_DK ** -0.5)
    kh = heads(k, GLA_DK)
    vh = heads(v, GLA_DV)
    b = jnp.cumsum(heads(log_a, GLA_DK), axis=3)
    b_last = b[:, :, :, -1:, :]

    q_e = qh * jnp.exp(b)
    k_e = kh * jnp.exp(-b)
    k_d = kh * jnp.exp(b_last - b)

    mask = jnp.tril(jnp.ones((GLA_CHUNK, GLA_CHUNK), dtype=bool))
    scores = jnp.einsum('bhcid,bhcjd->bhcij', q_e, k_e)
    scores = jnp.where(mask, scores, 0.0)
    o_intra = jnp.einsum('bhcij,bhcjv->bhciv', scores, vh)

    upd = jnp.einsum('bhcld,bhclv->bhcdv', k_d, vh)
    decay = jnp.exp(b_last[:, :, :, 0, :])

    def step(state, inp):
        dec, u = inp
        return dec[..., None] * state + u, state

    s0 = jnp.zeros((bsz, GLA_HEADS, GLA_DK, GLA_DV), f32)
    _, s_prev = lax.scan(step, s0, (jnp.moveaxis(decay, 2, 0), jnp.moveaxis(upd, 2, 0)))
    s_prev = jnp.moveaxis(s_prev, 0, 2)
    o_inter = jnp.einsum('bhcld,bhcdv->bhclv', q_e, s_prev)

    o = (o_intra + o_inter).transpose(0, 2, 3, 1, 4).reshape(bsz, seq, GLA_HEADS, GLA_DV)
    o = o * lax.rsqrt(jnp.mean(o * o, axis=-1, keepdims=True) + EPS) * norm_w.astype(f32)
    return o.reshape(bsz, seq, GLA_WIDTH) * jax.nn.silu(g.astype(f32))


def _complex_scan_combine(left, right):
    a1r, a1i, b1r, b1i = left
    a2r, a2i, b2r, b2i = right
    ar = a2r * a1r - a2i * a1i
    ai = a2r * a1i + a2i * a1r
    br = a2r * b1r - a2i * b1i + b2r
    bi = a2r * b1i + a2i * b1r + b2i
    return (ar, ai, br, bi)


def s5_mixer(u, a_re, a_im, log_dt, b_re, b_im, c_re, c_im, d_skip, w_glu, b_glu):
    f32 = jnp.float32
    bsz, seq, _ = u.shape
    ar = a_re.astype(f32)
    ai = a_im.astype(f32)
    dt = jnp.exp(log_dt.astype(f32))[:, None]
    mag = jnp.exp(ar * dt)
    abr = mag * jnp.cos(ai * dt)
    abi = mag * jnp.sin(ai * dt)
    den = ar * ar + ai * ai
    nr = abr - 1.0
    fr = (nr * ar + abi * ai) / den
    fi = (abi * ar - nr * ai) / den
    br_ = b_re.astype(f32)
    bi_ = b_im.astype(f32)
    bbr = fr[..., None] * br_ - fi[..., None] * bi_
    bbi = fr[..., None] * bi_ + fi[..., None] * br_

    ut = jnp.swapaxes(u.astype(f32).reshape(bsz, seq, S5_GROUPS, S5_GROUP), 0, 1)
    bu_r = jnp.einsum('sbgp,gnp->sbgn', ut, bbr)
    bu_i = jnp.einsum('sbgp,gnp->sbgn', ut, bbi)
    a_r = jnp.broadcast_to(abr[None, None], (seq, 1, S5_GROUPS, S5_STATE))
    a_i = jnp.broadcast_to(abi[None, None], (seq, 1, S5_GROUPS, S5_STATE))
    _, _, xr, xi = lax.associative_scan(_complex_scan_combine, (a_r, a_i, bu_r, bu_i), axis=0)

    y = (jnp.einsum('sbgn,gpn->sbgp', xr, c_re.astype(f32))
         - jnp.einsum('sbgn,gpn->sbgp', xi, c_im.astype(f32))
         + d_skip.astype(f32) * ut)
    y = jnp.swapaxes(y, 0, 1).reshape(bsz, seq, S5_WIDTH)
    z = jax.nn.gelu(y)
    return z * jax.nn.sigmoid(z @ w_glu.astype(f32) + b_glu.astype(f32))


def setup_inputs(seed: int = 0) -> dict:
    key = jax.random.key(seed)
    ks = jax.random.split(key, 24)
    f32 = jnp.float32
    L = DEPTH
    nrm = lambda k, shape, scale: jax.random.normal(k, shape, f32) * scale
    gain = lambda k, shape: 1.0 + 0.01 * jax.random.normal(k, shape, f32)
    x = jax.random.normal(ks[0], (BATCH, SEQ, D_MODEL), f32)
    norm_mix_w = gain(ks[1], (L, D_MODEL))
    w_in = nrm(ks[2], (L, D_MODEL, IN_COLS), D_MODEL ** -0.5)
    w_gk_up = nrm(ks[3], (L, GLA_GATE_RANK, GLA_KEY_DIM), GLA_GATE_RANK ** -0.5)
    b_gk = nrm(ks[4], (L, GLA_KEY_DIM), 0.01)
    gla_norm_w = gain(ks[5], (L, GLA_DV))
    s5_a_re = -0.5 + nrm(ks[6], (L, S5_GROUPS, S5_STATE), 0.01)
    s5_a_im = (math.pi * jnp.arange(S5_STATE, dtype=f32))[None, None, :] + nrm(ks[7], (L, S5_GROUPS, S5_STATE), 0.01)
    s5_log_dt = jax.random.uniform(ks[8], (L, S5_GROUPS), f32, math.log(S5_DT_MIN), math.log(S5_DT_MAX))
    b_scale = (2.0 * S5_GROUP) ** -0.5
    s5_b_re = nrm(ks[9], (L, S5_GROUPS, S5_STATE, S5_GROUP), b_scale)
    s5_b_im = nrm(ks[10], (L, S5_GROUPS, S5_STATE, S5_GROUP), b_scale)
    c_scale = (2.0 * S5_STATE) ** -0.5
    s5_c_re = nrm(ks[11], (L, S5_GROUPS, S5_GROUP, S5_STATE), c_scale)
    s5_c_im = nrm(ks[12], (L, S5_GROUPS, S5_GROUP, S5_STATE), c_scale)
    s5_d = nrm(ks[13], (L, S5_GROUPS, S5_GROUP), 1.0)
    w_glu = nrm(ks[14], (L, S5_WIDTH, S5_WIDTH), S5_WIDTH ** -0.5)
    b_glu = nrm(ks[15], (L, S5_WIDTH), 0.01)
    w_out = nrm(ks[16], (L, MIX_WIDTH, D_MODEL), MIX_WIDTH ** -0.5)
    norm_mlp_w = gain(ks[17], (L, D_MODEL))
    w_mlp_up = nrm(ks[18], (L, D_MODEL, D_FF), D_MODEL ** -0.5)
    w_mlp_down = nrm(ks[19], (L, D_FF, D_MODEL), D_FF ** -0.5)
    norm_final_w = gain(ks[20], (D_MODEL,))
    return {
        "x": x, "norm_mix_w": norm_mix_w, "w_in": w_in, "w_gk_up": w_gk_up, "b_gk": b_gk,
        "gla_norm_w": gla_norm_w, "s5_a_re": s5_a_re, "s5_a_im": s5_a_im, "s5_log_dt": s5_log_dt,
        "s5_b_re": s5_b_re, "s5_b_im": s5_b_im, "s5_c_re": s5_c_re, "s5_c_im": s5_c_im,
        "s5_d": s5_d, "w_glu": w_glu, "b_glu": b_glu, "w_out": w_out, "norm_mlp_w": norm_mlp_w,
        "w_mlp_up": w_mlp_up, "w_mlp_down": w_mlp_down, "norm_final_w": norm_final_w,
    }


def reference(x, norm_mix_w, w_in, w_gk_up, b_gk, gla_norm_w, s5_a_re, s5_a_im, s5_log_dt,
              s5_b_re, s5_b_im, s5_c_re, s5_c_im, s5_d, w_glu, b_glu, w_out, norm_mlp_w,
              w_mlp_up, w_mlp_down, norm_final_w):
    for l in range(DEPTH):
        h = rmsnorm(x, norm_mix_w[l])
        p = h @ w_in[l]
        q, k, v, gk_lr, g, u = jnp.split(p, IN_SPLITS, axis=-1)
        o_gla = gla_mixer(q, k, v, gk_lr, g, w_gk_up[l], b_gk[l], gla_norm_w[l])
        o_s5 = s5_mixer(u, s5_a_re[l], s5_a_im[l], s5_log_dt[l], s5_b_re[l], s5_b_im[l],
                        s5_c_re[l], s5_c_im[l], s5_d[l], w_glu[l], b_glu[l])
        mix = jnp.concatenate([o_gla, o_s5], axis=-1).astype(x.dtype)
        x = x + mix @ w_out[l]
        h = rmsnorm(x, norm_mlp_w[l])
        x = x + jnp.square(jax.nn.relu(h @ w_mlp_up[l])) @ w_mlp_down[l]
    return rmsnorm(x, norm_final_w)
```

```python
import contextlib
import math
import numpy as np
import concourse.bass as bass
import concourse.mybir as mybir
from concourse.bass_utils import run_bass_kernel_spmd

F32 = mybir.dt.float32
BF16 = mybir.dt.bfloat16
I32 = mybir.dt.int32
AF = mybir.ActivationFunctionType
ALU = mybir.AluOpType
EPS = 1e-6
NCORES = 8


class Trk:
    def __init__(s, nc, es):
        s.nc, s.es = nc, es
        s.E = dict(pe=nc.tensor, act=nc.scalar, dve=nc.vector, pool=nc.gpsimd, sp=nc.sync)
        s.cur = {}
        s.seen = {e: {} for e in s.E}
        s.res = {}
        s.dsem = {}
        s.nsem = 0
        s.allsems = []

    def _newsem(s, name):
        s.nsem += 1
        sem = s.es.enter_context(s.nc.semaphore(f"{name}{s.nsem}"))
        ent = [sem, 0, s.nsem]
        s.allsems.append(ent)
        return ent

    def _wait(s, e, tok):
        ent, val, weng = tok
        if weng == 'pe' and e == 'pe':
            return
        if s.seen[e].get(ent[2], 0) >= val:
            return
        s.E[e].wait_ge(ent[0], val)
        s.seen[e][ent[2]] = val

    def _deps(s, e, reads, writes):
        for k in reads:
            r = s.res.get(k)
            if r and r[0]:
                s._wait(e, r[0])
        for k in writes:
            r = s.res.get(k)
            if r:
                if r[0] and r[0][2] != e:
                    s._wait(e, r[0])
                for t in r[1].values():
                    if t[2] != e:
                        s._wait(e, t)

    def _commit(s, tok, reads, writes):
        for k in reads:
            r = s.res.setdefault(k, [None, {}])
            r[1][tok[0][2]] = tok
        for k in writes:
            s.res[k] = [tok, {}]

    def op(s, e, reads, writes, fn):
        s._deps(e, reads, writes)
        ins = fn()
        c = s.cur.get(e)
        if c is None or c[1] >= 30000:
            c = s._newsem(e)
            s.cur[e] = c
        c[1] += 1
        ins.then_inc(c[0], 1)
        s._commit((c, c[1], e), reads, writes)

    def dma(s, q, out, in_, key, reads=None, writes=None, **kw):
        reads = [in_.name] if reads is None else reads
        writes = [out.name] if writes is None else writes
        s._deps(q, reads, writes)
        d = s.dsem.get(key)
        if d is None:
            d = s._newsem("d")
            s.dsem[key] = d
        d[1] += 16
        s.E[q].dma_start(out=out, in_=in_, **kw).then_inc(d[0], 16)
        s._commit((d, d[1], 'dma'), reads, writes)

    def barrier(s, engines=('pe', 'act', 'dve', 'pool', 'sp')):
        for e in engines:
            for ent in s.allsems:
                if ent[1] > 0 and s.seen[e].get(ent[2], 0) < ent[1]:
                    s.E[e].wait_ge(ent[0], ent[1])
                    s.seen[e][ent[2]] = ent[1]


def _names(*aps):
    return [a.name for a in aps if hasattr(a, "name")]


class B:
    def __init__(s, t):
        s.t = t
        s.nc = t.nc

    def act(s, out, in_, func, scale=None, bias=None, accum_out=None, eng='act'):
        kw = {}
        if scale is not None:
            kw['scale'] = scale
        if bias is not None:
            kw['bias'] = bias
        if accum_out is not None:
            kw['accum_out'] = accum_out
        s.t.op('act', _names(in_, scale, bias), _names(out, accum_out),
               lambda: s.nc.scalar.activation(out=out, in_=in_, func=func, **kw))

    def tt(s, out, a, b, op, e='dve'):
        s.t.op(e, _names(a, b), _names(out),
               lambda: s.t.E[e].tensor_tensor(out=out, in0=a, in1=b, op=op))

    def ts(s, out, a, s1, op0, s2=None, op1=None, e='dve'):
        kw = {} if op1 is None else {'op1': op1}
        s.t.op(e, _names(a, s1, s2), _names(out),
               lambda: s.t.E[e].tensor_scalar(out=out, in0=a, scalar1=s1, scalar2=s2, op0=op0, **kw))

    def stt(s, out, a, sc, b, op0, op1):
        s.t.op('dve', _names(a, sc, b), _names(out),
               lambda: s.nc.vector.scalar_tensor_tensor(out=out, in0=a, scalar=sc, in1=b, op0=op0, op1=op1))

    def cp(s, out, in_, e='dve'):
        s.t.op(e, _names(in_), _names(out), lambda: s.t.E[e].tensor_copy(out=out, in_=in_))

    def recip(s, out, in_):
        s.t.op('dve', _names(in_), _names(out), lambda: s.nc.vector.reciprocal(out=out, in_=in_))

    def memset(s, ap, v, e='dve'):
        s.t.op(e, [], _names(ap), lambda: s.t.E[e].memset(ap, v))

    def scan(s, out, d0, d1, init=0.0):
        s.t.op('dve', _names(d0, d1, init), _names(out),
               lambda: s.nc.vector.tensor_tensor_scan(out=out, data0=d0, data1=d1, initial=init,
                                                      op0=ALU.mult, op1=ALU.add))

    def mm(s, ops):
        reads, writes = [], []
        for o in ops:
            reads += _names(o['lhsT'], o['rhs'])
            writes += _names(o['out'])
        reads, writes = list(dict.fromkeys(reads)), list(dict.fromkeys(writes))
        n = len(ops)

        def fn():
            ins = None
            for o in ops:
                kw = {}
                if o.get('tp') is not None:
                    kw['tile_position'] = o['tp']
                ins = s.nc.tensor.matmul(o['out'], lhsT=o['lhsT'], rhs=o['rhs'],
                                         start=o['start'], stop=o['stop'], **kw)
            return ins
        s.t.op('pe', reads, writes, fn)

    def tr(s, outs_ins, ident):
        reads, writes = _names(ident), []
        for o, i in outs_ins:
            reads += _names(i)
            writes += _names(o)

        def fn():
            ins = None
            for o, i in outs_ins:
                ins = s.nc.tensor.transpose(o, i, ident)
            return ins
        s.t.op('pe', list(dict.fromkeys(reads)), list(dict.fromkeys(writes)), fn)


def build(NSEQ, S, dbg=False):
    nc = bass.Bass("TRN2", target_bir_lowering=False)
    NT = NSEQ * S
    BLK = 256
    NTI = BLK // 128
    CH = BLK // 8
    NB = S // BLK
    NBLK = NSEQ * NB

    def D(name, shape, kind="ExternalInput"):
        return nc.dram_tensor(name, shape, F32, kind=kind).ap()
    x = D("x", [NT, 1024])
    norm_mix_w = D("norm_mix_w", [1024])
    w_in = D("w_in", [1024, 2064])
    w_gk_up = D("w_gk_up", [16, 256])
    b_gk = D("b_gk", [256])
    gla_norm_w = D("gla_norm_w", [128])
    a_re = D("s5_a_re", [32, 64])
    a_im = D("s5_a_im", [32, 64])
    log_dt = D("s5_log_dt", [32])
    b_re = D("s5_b_re", [32, 64, 16])
    b_im = D("s5_b_im", [32, 64, 16])
    c_re = D("s5_c_re", [32, 16, 64])
    c_im = D("s5_c_im", [32, 16, 64])
    s5_d = D("s5_d", [32, 16])
    w_glu = D("w_glu", [512, 512])
    b_glu = D("b_glu", [512])
    w_out = D("w_out", [1024, 1024])
    norm_mlp_w = D("norm_mlp_w", [1024])
    w_up = D("w_mlp_up", [1024, 4096])
    w_down = D("w_mlp_down", [4096, 1024])
    norm_final_w = D("norm_final_w", [1024])
    c_ident = D("c_ident", [128, 128])
    c_cmask = D("c_cmask", [128, 512])
    c_umask = D("c_umask", [128, 128])
    c_rmask = D("c_rmask", [128, BLK])
    c_reset = D("c_reset", [128, 16 * CH])
    out = D("out", [NT, 1024], kind="ExternalOutput")
    x1 = D("x1s", [NT, 1024], kind="ExternalOutput" if dbg else "Internal")

    es = contextlib.ExitStack()
    with es:
        t = Trk(nc, es)
        b = B(t)

        def SB(name, shape, dt=F32, stack=es):
            return stack.enter_context(nc.sbuf_tensor(name, shape, dt))

        def PS(name, shape, dt=F32):
            return es.enter_context(nc.psum_tensor(name, shape, dt))

        pT = PS("pT", [128, 1024], BF16)
        pO = PS("pO", [128, 1024])
        p1 = PS("p1", [128, 512])
        p2 = PS("p2", [128, 512])
        p3 = PS("p3", [128, 512])
        p4 = PS("p4", [128, 512])
        p5 = PS("p5", [128, 512])

        ident = SB("ident", [128, 128])
        identb = SB("identb", [128, 128], BF16)
        t.dma('sp', ident[:], c_ident[:, :], 'ident')
        b.cp(identb[:], ident[:])
        epsc = SB("epsc", [128, 1])
        b.memset(epsc[:], EPS)

        esA = contextlib.ExitStack()
        with esA:
            def SA(name, shape, dt=F32):
                return SB(name, shape, dt, stack=esA)

            esS = contextlib.ExitStack()
            Kbd = SA("Kbd", [128, 8, 4, 128], BF16)
            ZW = SA("ZW", [128, 4, 8, 2, 2, 128], BF16)
            OgR = SA("OgR", [128, 8, 16, 32], BF16)
            OgI = SA("OgI", [128, 8, 16, 32], BF16)
            Cc = SA("Cc", [128, 16, CH])
            Sn = SA("Sn", [128, 16, CH])
            R8m = SA("R8m", [128, 16, CH])
            L8r = SA("L8r", [128, 16])
            L8i = SA("L8i", [128, 16])
            with esS:
                def SS(name, shape, dt=F32):
                    return SB(name, shape, dt, stack=esS)
                are = SS("are", [128, 16]); aim = SS("aim", [128, 16]); ldt = SS("ldt", [128, 16])
                for h in range(2):
                    t.dma('sp', are[64 * h:64 * h + 64, :], a_re.rearrange("(q h) n -> h n q", h=2)[h], f'are{h}',
                          allow_slow_non_contiguous=True)
                    t.dma('sp', aim[64 * h:64 * h + 64, :], a_im.rearrange("(q h) n -> h n q", h=2)[h], f'aim{h}',
                          allow_slow_non_contiguous=True)
                    t.dma('sp', ldt[64 * h:64 * h + 64, :],
                          log_dt.rearrange("(q h) -> h q", h=2)[h].partition_broadcast(64), f'ldt{h}',
                          allow_slow_non_contiguous=True)
                Br = SS("Br", [128, 16, 16]); Bi = SS("Bi", [128, 16, 16])
                for h in range(2):
                    t.dma('sp', Br[64 * h:64 * h + 64], b_re.rearrange("(q h) n p -> h n q p", h=2)[h], f'Br{h}')
                    t.dma('sp', Bi[64 * h:64 * h + 64], b_im.rearrange("(q h) n p -> h n q p", h=2)[h], f'Bi{h}')
                Cr = SS("Cr", [128, 16, 16]); Ci = SS("Ci", [128, 16, 16])
                Cnat = SS("Cnat", [128, 128])
                for (src, dst, nm) in ((c_re, Cr, 'cr'), (c_im, Ci, 'ci')):
                    for qb in range(2):
                        for ql in range(8):
                            t.dma('sp', Cnat[16 * ql:16 * ql + 16, :].rearrange("p (h n) -> p h n", h=2),
                                  src.rearrange("(q h) p n -> q p h n", h=2)[qb * 8 + ql], f'cn{ql}')
                        b.tr([(p1[:, 0:128], Cnat[:])], ident[:])
                        b.cp(dst[:, qb * 8:(qb + 1) * 8, :], p1[:, 0:128].rearrange("p (q c) -> p q c", c=16))
                dcol = SS("dcol", [128, 4])
                t.dma('sp', dcol[:], s5_d.rearrange("(t g) p -> (g p) t", t=4), 'dcol', allow_slow_non_contiguous=True)

                cnt = [0]

                def T(shape=(128, 16)):
                    cnt[0] += 1
                    return SS(f"tmp{cnt[0]}", list(shape))
                dtt = T(); adt = T(); th = T(); mag = T()
                b.act(dtt[:], ldt[:], AF.Exp)
                b.tt(adt[:], are[:], dtt[:], ALU.mult)
                b.tt(th[:], aim[:], dtt[:], ALU.mult)
                b.act(mag[:], adt[:], AF.Exp)
                r8 = T()
                b.act(r8[:], adt[:], AF.Exp, scale=8.0)
                kq = T(); ki = SS("ki", [128, 16], I32); kf = T(); rr = T()
                b.ts(kq[:], th[:], 1.0 / (2 * math.pi), ALU.mult)
                b.cp(ki[:], kq[:])
                b.cp(kf[:], ki[:])
                b.stt(rr[:], kf[:], -2 * math.pi, th[:], ALU.mult, ALU.add)
                s2 = T(); s4 = T(); c2 = T(); sinr = T(); cosr = T(); tq = T()
                b.act(s2[:], rr[:], AF.Sin, scale=0.5)
                b.act(s4[:], rr[:], AF.Sin, scale=0.25)
                b.tt(tq[:], s4[:], s4[:], ALU.mult)
                b.ts(c2[:], tq[:], -2.0, ALU.mult, 1.0, ALU.add)
                b.tt(sinr[:], s2[:], c2[:], ALU.mult)
                b.ts(sinr[:], sinr[:], 2.0, ALU.mult)
                b.tt(tq[:], s2[:], s2[:], ALU.mult)
                b.ts(cosr[:], tq[:], -2.0, ALU.mult, 1.0, ALU.add)
                LPr = SS("LPr", [128, 9, 16]); LPi = SS("LPi", [128, 9, 16])
                b.memset(LPr[:, 0, :], 1.0)
                b.memset(LPi[:, 0, :], 0.0)
                b.tt(LPr[:, 1, :], mag[:], cosr[:], ALU.mult)
                b.tt(LPi[:, 1, :], mag[:], sinr[:], ALU.mult)
                t1 = T(); t2 = T()

                def cmul(orr, oi, ar_, ai_, br_, bi_, ta, tb):
                    b.tt(ta, ar_, br_, ALU.mult)
                    b.tt(tb, ai_, bi_, ALU.mult)
                    b.tt(orr, ta, tb, ALU.subtract)
                    b.tt(ta, ar_, bi_, ALU.mult)
                    b.tt(tb, ai_, br_, ALU.mult)
                    b.tt(oi, ta, tb, ALU.add)
                for k in range(2, 9):
                    cmul(LPr[:, k, :], LPi[:, k, :], LPr[:, k - 1, :], LPi[:, k - 1, :], LPr[:, 1, :], LPi[:, 1, :],
                         t1[:], t2[:])
                b.cp(L8r[:], LPr[:, 8, :])
                b.cp(L8i[:], LPi[:, 8, :])
                den = T(); rden = T(); nr = T(); fr = T(); fi = T()
                b.tt(den[:], are[:], are[:], ALU.mult)
                b.tt(t1[:], aim[:], aim[:], ALU.mult)
                b.tt(den[:], den[:], t1[:], ALU.add)
                b.recip(rden[:], den[:])
                b.ts(nr[:], LPr[:, 1, :], -1.0, ALU.add)
                b.tt(t1[:], nr[:], are[:], ALU.mult)
                b.tt(t2[:], LPi[:, 1, :], aim[:], ALU.mult)
                b.tt(fr[:], t1[:], t2[:], ALU.add)
                b.tt(fr[:], fr[:], rden[:], ALU.mult)
                b.tt(t1[:], LPi[:, 1, :], are[:], ALU.mult)
                b.tt(t2[:], nr[:], aim[:], ALU.mult)
                b.tt(fi[:], t1[:], t2[:], ALU.subtract)
                b.tt(fi[:], fi[:], rden[:], ALU.mult)
                Bbr = SS("Bbr", [128, 16, 16]); Bbi = SS("Bbi", [128, 16, 16])
                u1 = SS("u1", [128, 16, 16]); u2 = SS("u2", [128, 16, 16])

                def bc(a, n=16):
                    return a.unsqueeze(2).to_broadcast([128, 16, n])
                cmul(Bbr[:], Bbi[:], bc(fr[:]), bc(fi[:]), Br[:], Bi[:], u1[:], u2[:])
                BPr = SS("BPr", [128, 16, 128]); BPi = SS("BPi", [128, 16, 128])
                CPr = SS("CPr", [128, 16, 128]); CPi = SS("CPi", [128, 16, 128])
                for (pad, src) in ((BPr, Bbr), (BPi, Bbi), (CPr, Cr), (CPi, Ci)):
                    b.memset(pad[:], 0.0)
                    for jp in range(4):
                        for h in range(2):
                            c0 = 32 * jp + 16 * h
                            b.cp(pad[64 * h:64 * h + 64, jp::4, c0:c0 + 16], src[64 * h:64 * h + 64, jp::4, :])
                GPr = SS("GPr", [128, 16, 128]); GPi = SS("GPi", [128, 16, 128])
                EPr = SS("EPr", [128, 16, 128]); EPi = SS("EPi", [128, 16, 128])
                v1 = SS("v1", [128, 16, 128]); v2 = SS("v2", [128, 16, 128])
                b.memset(Kbd[:], 0.0)
                for k in range(9):
                    lr, li = bc(LPr[:, k, :], 128), bc(LPi[:, k, :], 128)
                    cmul(EPr[:], EPi[:], CPr[:], CPi[:], lr, li, v1[:], v2[:])
                    if k >= 1:
                        for q in range(16):
                            c0 = 32 * (q % 4)
                            b.cp(OgR[:, k - 1, q, :], EPr[:, q, c0:c0 + 32], e='pool')
                            b.ts(OgI[:, k - 1, q, :], EPi[:, q, c0:c0 + 32], -1.0, ALU.mult, e='pool')
                    if k <= 7:
                        b.ts(v1[:], EPi[:], -1.0, ALU.mult)
                        for tt_ in range(4):
                            ops = []
                            for jp in range(4):
                                q = 4 * tt_ + jp
                                ops.append(dict(out=p2[:, 0:128], lhsT=BPr[:, q, :], rhs=EPr[:, q, :],
                                                start=(jp == 0), stop=False))
                                ops.append(dict(out=p2[:, 0:128], lhsT=BPi[:, q, :], rhs=v1[:, q, :],
                                                start=False, stop=(jp == 3)))
                            b.mm(ops)
                            if k == 0:
                                b.stt(Kbd[:, 0, tt_, :], ident[:], dcol[:, tt_:tt_ + 1], p2[:, 0:128],
                                      ALU.mult, ALU.add)
                            else:
                                b.cp(Kbd[:, k, tt_, :], p2[:, 0:128])
                        cmul(GPr[:], GPi[:], BPr[:], BPi[:], lr, li, v1[:], v2[:])
                        js = 7 - k
                        for tt_ in range(4):
                            for jj in range(2):
                                for ri, G in ((0, GPr), (1, GPi)):
                                    b.mm([dict(out=p3[:, 0:128], lhsT=G[:, 4 * tt_ + jj, :], rhs=ident[:],
                                               start=True, stop=False),
                                          dict(out=p3[:, 0:128], lhsT=G[:, 4 * tt_ + 2 + jj, :], rhs=ident[:],
                                               start=False, stop=True)])
                                    b.act(ZW[:, tt_, js, jj, ri, :], p3[:, 0:128], AF.Copy)
                p8r = T(); p8i = T(); q8r = T(); q8i = T()
                b.cp(p8r[:], cosr[:]); b.cp(p8i[:], sinr[:])
                for _ in range(3):
                    cmul(q8r[:], q8i[:], p8r[:], p8i[:], p8r[:], p8i[:], t1[:], t2[:])
                    b.cp(p8r[:], q8r[:]); b.cp(p8i[:], q8i[:])
                b.memset(Cc[:, :, 0:1], 1.0)
                b.memset(Sn[:, :, 0:1], 0.0)
                w1 = SS("w1", [128, 16, 32]); w2 = SS("w2", [128, 16, 32])
                for lv in range(int(math.log2(CH))):
                    m = 1 << lv
                    pr = p8r[:].unsqueeze(2).to_broadcast([128, 16, m])
                    pi_ = p8i[:].unsqueeze(2).to_broadcast([128, 16, m])
                    cmul(Cc[:, :, m:2 * m], Sn[:, :, m:2 * m], Cc[:, :, 0:m], Sn[:, :, 0:m], pr, pi_,
                         w1[:, :, 0:m], w2[:, :, 0:m])
                    cmul(q8r[:], q8i[:], p8r[:], p8i[:], p8r[:], p8i[:], t1[:], t2[:])
                    b.cp(p8r[:], q8r[:]); b.cp(p8i[:], q8i[:])
                rst = SS("rst", [128, 16, CH])
                t.dma('sp', rst[:], c_reset.rearrange("p (q c) -> p q c", c=CH), 'rst')
                b.tt(R8m[:], rst[:], r8[:].unsqueeze(2).to_broadcast([128, 16, CH]), ALU.mult)
                t.barrier()
            cmask = SA("cmask", [128, 512])
            umask = SA("umask", [128, 128])
            rmask = SA("rmask", [128, BLK])
            t.dma('sp', cmask[:], c_cmask[:, :], 'cmask')
            t.dma('sp', umask[:], c_umask[:, :], 'umask')
            t.dma('sp', rmask[:], c_rmask[:, :], 'rmask')

            nmw = SA("nmw", [128, 8])
            Win = SA("Win", [128, 8, 2064], BF16)
            Wout = SA("Wout", [128, 8, 1024], BF16)
            Wglu = SA("Wglu", [128, 4, 512], BF16)
            bglu = SA("bglu", [128, 4])
            WG = SA("WG", [17, 256])
            gnw = SA("gnw", [128, 512])
            gkl = SA("gkl", [17, BLK])
            esW = contextlib.ExitStack()
            with esW:
                stg = [SB("stg0", [128, 2064], F32, stack=esW), SB("stg1", [128, 2064], F32, stack=esW)]
                t.dma('sp', nmw[:], norm_mix_w.rearrange("(k p) -> p k", p=128), 'nmw',
                      allow_slow_non_contiguous=True)
                for k in range(8):
                    t.dma('sp', stg[k % 2][:], w_in[128 * k:128 * (k + 1), :], f'stg{k % 2}')
                    b.act(Win[:, k, :], stg[k % 2][:], AF.Copy, scale=nmw[:, k:k + 1])
                for k in range(8):
                    t.dma('sp', stg[k % 2][:, 0:1024], w_out[128 * k:128 * (k + 1), :], f'stg{k % 2}')
                    b.cp(Wout[:, k, :], stg[k % 2][:, 0:1024])
                for k in range(4):
                    t.dma('sp', stg[k % 2][:, 0:512], w_glu[128 * k:128 * (k + 1), :], f'stg{k % 2}')
                    b.cp(Wglu[:, k, :], stg[k % 2][:, 0:512])
                t.dma('sp', bglu[:], b_glu.rearrange("(m p) -> p m", p=128), 'bglu', allow_slow_non_contiguous=True)
                t.dma('sp', WG[0:16, :], w_gk_up[:, :], 'WG')
                t.dma('sp', WG[16:17, :], b_gk.rearrange("(o n) -> o n", o=1), 'WGb')
                for h in range(4):
                    t.dma('sp', gnw[:, 128 * h:128 * (h + 1)], gla_norm_w.partition_broadcast(128), f'gnw{h}')
                b.memset(gkl[:], 1.0)
                t.barrier()
            xt = [SA(f"xt{i}", [128, 1024]) for i in range(NTI)]
            ss = SA("ss", [128, 4]); rstd = SA("rstd", [128, 4])
            hb = SA("hb", [128, 1024], BF16)
            hT = SA("hT", [128, 8, BLK], BF16)
            la = SA("la", [64, BLK]); cT = SA("cT", [64, BLK])
            eb = SA("eb", [64, BLK]); enb = SA("enb", [64, BLK]); dec = SA("dec", [64, 4, NTI])
            qe = SA("qe", [64, 4, BLK], BF16); ke = SA("ke", [64, 4, BLK], BF16)
            uT = SA("uT", [128, 4, BLK], BF16)
            vtm = SA("vtm", [128, 512], BF16)
            latm = SA("latm", [128, 256]); edt = SA("edt", [128, 256])
            kd = SA("kd", [128, 256], BF16)
            sg = SA("sg", [128, 512])
            scT = SA("scT", [128, 512], BF16)
            Sg = SA("Sg", [64, 4, 128]); Sgb = SA("Sgb", [64, 4, 128], BF16)
            ss4 = SA("ss4", [128, 4]); rs4 = SA("rs4", [128, 4]); sq2 = SA("sq2", [128, 128], BF16)
            mixtm = SA("mixtm", [128, 512], BF16)
            mixT = SA("mixT", [128, 8, BLK], BF16)
            Zsb = SA("Zsb", [128, 16, 2, CH])
            zr = SA("zr", [128, 16, CH]); zi = SA("zi", [128, 16, CH])
            m1 = SA("m1", [128, 16, CH]); m2 = SA("m2", [128, 16, CH])
            Srb = SA("Srb", [128, 16, CH + 1], BF16); Sib = SA("Sib", [128, 16, CH + 1], BF16)
            car = SA("car", [128, 16]); cai = SA("cai", [128, 16])
            c1 = SA("c1", [128, 16]); c2_ = SA("c2", [128, 16]); c3 = SA("c3", [128, 16])
            zT = SA("zT", [128, 4, BLK], BF16)
            sgl = SA("sgl", [128, BLK])

            for blk in range(NBLK):
                tok0 = blk * BLK
                if blk % NB == 0:
                    b.memset(Sg[:], 0.0); b.memset(Sgb[:], 0.0)
                    b.memset(car[:], 0.0); b.memset(cai[:], 0.0)
                    b.memset(Srb[:], 0.0); b.memset(Sib[:], 0.0)
                for i in range(NTI):
                    t.dma('sp', xt[i][:], x[tok0 + 128 * i: tok0 + 128 * (i + 1), :], f'xt{i}')
                for i in range(NTI):
                    b.act(hb[:], xt[i][:], AF.Square, accum_out=ss[:, i:i + 1])
                    b.act(rstd[:, i:i + 1], ss[:, i:i + 1], AF.Sqrt, scale=1.0 / 1024, bias=epsc[:, 0:1])
                    b.recip(rstd[:, i:i + 1], rstd[:, i:i + 1])
                    b.act(hb[:], xt[i][:], AF.Copy, scale=rstd[:, i:i + 1])
                    b.tr([(pT[:, 128 * k:128 * (k + 1)], hb[:, 128 * k:128 * (k + 1)]) for k in range(8)], identb[:])
                    b.cp(hT[:, :, 128 * i:128 * (i + 1)], pT[:].rearrange("p (k c) -> p k c", c=128))

                def proj_fm(ps, c0, M):
                    b.mm([dict(out=ps, lhsT=Win[:, k, c0:c0 + M], rhs=hT[:, k, :], start=(k == 0), stop=(k == 7))
                          for k in range(8)])
                proj_fm(p1[0:16, 0:BLK], 1024, 16)
                b.cp(gkl[0:16, :], p1[0:16, 0:BLK])
                for h in range(4):
                    b.mm([dict(out=p2[0:64, 0:BLK], lhsT=WG[:, 64 * h:64 * h + 64], rhs=gkl[:], start=True, stop=True)])
                    b.act(la[:], p2[0:64, 0:BLK], AF.Exp, scale=-1.0)
                    b.act(la[:], la[:], AF.Ln, bias=1.0)
                    b.scan(cT[:], rmask[0:64, :], la[:])
                    b.act(eb[:], cT[:], AF.Exp, scale=-1.0 / 16)
                    b.act(enb[:], cT[:], AF.Exp, scale=1.0 / 16)
                    b.cp(dec[:, h, :], eb[:, 127::128])
                    proj_fm(p1[0:64, 0:BLK], 64 * h, 64)
                    b.stt(qe[:, h, :], p1[0:64, 0:BLK], 0.125, eb[:], ALU.mult, ALU.mult)
                    proj_fm(p2[0:64, 0:BLK], 256 + 64 * h, 64)
                    b.tt(ke[:, h, :], p2[0:64, 0:BLK], enb[:], ALU.mult)
                for tt_ in range(4):
                    proj_fm(p1[:, 0:BLK], 1552 + 128 * tt_, 128)
                    b.act(uT[:, tt_, :], p1[:, 0:BLK], AF.Copy)
                for i in range(NTI):
                    cs = slice(128 * i, 128 * (i + 1))

                    def proj_tm(ps, c0, N):
                        b.mm([dict(out=ps, lhsT=hT[:, k, cs], rhs=Win[:, k, c0:c0 + N], start=(k == 0), stop=(k == 7))
                              for k in range(8)])
                    b.mm([dict(out=p4[:, 256:512], lhsT=gkl[:, cs], rhs=WG[:], start=True, stop=True)])
                    b.act(latm[:], p4[:, 256:512], AF.Exp, scale=-1.0)
                    b.act(latm[:], latm[:], AF.Ln, bias=1.0)
                    b.mm([dict(out=p5[:, 0:256], lhsT=umask[:], rhs=latm[:], start=True, stop=True)])
                    b.act(edt[:], p5[:, 0:256], AF.Exp, scale=-1.0 / 16)
                    proj_tm(p4[:, 0:256], 256, 256)
                    b.tt(kd[:], p4[:, 0:256], edt[:], ALU.mult)
                    proj_tm(p3[:], 512, 512)
                    b.act(vtm[:], p3[:], AF.Copy)
                    proj_tm(p3[:], 1040, 512)
                    b.act(sg[:], p3[:], AF.Silu)
                    b.tt(sg[:], sg[:], gnw[:], ALU.mult)
                    b.mm([dict(out=p5[:, 128 * h:128 * (h + 1)], lhsT=ke[:, h, cs], rhs=qe[:, h, cs],
                               start=True, stop=True) for h in range(4)])
                    b.tt(scT[:], p5[:], cmask[:], ALU.mult)
                    ops = []
                    for h in range(4):
                        hs = slice(128 * h, 128 * (h + 1))
                        ops.append(dict(out=p1[:, hs], lhsT=scT[:, hs], rhs=vtm[:, hs], start=True, stop=False))
                        ops.append(dict(out=p1[:, hs], lhsT=qe[:, h, cs], rhs=Sgb[:, h, :], start=False, stop=True))
                    b.mm(ops)
                    b.mm([dict(out=p2[0:64, 128 * h:128 * (h + 1)], lhsT=kd[:, 64 * h:64 * (h + 1)],
                               rhs=vtm[:, 128 * h:128 * (h + 1)], start=True, stop=True) for h in range(4)])
                    for h in range(4):
                        b.stt(Sg[:, h, :], Sg[:, h, :], dec[:, h, i:i + 1],
                              p2[0:64, 128 * h:128 * (h + 1)], ALU.mult, ALU.add)
                    b.act(Sgb[:], Sg[:], AF.Copy)
                    for h in range(4):
                        b.act(sq2[:], p1[:, 128 * h:128 * (h + 1)], AF.Square, accum_out=ss4[:, h:h + 1])
                    b.act(rs4[:], ss4[:], AF.Sqrt, scale=1.0 / 128, bias=epsc[:, 0:1])
                    b.recip(rs4[:], rs4[:])
                    for h in range(4):
                        hs = slice(128 * h, 128 * (h + 1))
                        b.stt(mixtm[:, hs], p1[:, hs], rs4[:, h:h + 1], sg[:, hs], ALU.mult, ALU.mult)
                    b.tr([(pT[:, 128 * h:128 * (h + 1)], mixtm[:, 128 * h:128 * (h + 1)]) for h in range(4)], identb[:])
                    b.cp(mixT[:, 0:4, cs], pT[:, 0:512].rearrange("p (k c) -> p k c", c=128))
                zbanks = {0: p3[:, 0:16 * CH], 1: pO[:, 0:16 * CH]}
                for H in range(2):
                    zb = zbanks[H].rearrange("p (t j r c) -> p t j r c", t=4, j=2, r=2)
                    ops = []
                    for tt_ in range(4):
                        uv = uT[64 * H:64 * H + 64, tt_, :].rearrange("p (c i) -> p c i", i=8)
                        for jj in range(2):
                            for ri in range(2):
                                for js in range(8):
                                    ops.append(dict(out=zb[:, tt_, jj, ri, :],
                                                    lhsT=ZW[64 * H:64 * H + 64, tt_, js, jj, ri, :],
                                                    rhs=uv[:, :, js], start=(js == 0), stop=(js == 7)))
                    b.mm(ops)
                Z5 = Zsb[:].rearrange("p (t j) r c -> p t j r c", j=4)
                for H in range(2):
                    zb = zbanks[H].rearrange("p (t j r c) -> p t j r c", t=4, j=2, r=2)
                    for tt_ in range(4):
                        b.act(Z5[:, tt_, 2 * H:2 * H + 2, :, :], zb[:, tt_, :, :, :], AF.Copy)
                Zr, Zi = Zsb[:, :, 0, :], Zsb[:, :, 1, :]
                b.tt(c1[:], L8r[:], car[:], ALU.mult)
                b.tt(c2_[:], L8i[:], cai[:], ALU.mult)
                b.tt(c1[:], c1[:], c2_[:], ALU.subtract)
                b.tt(c2_[:], L8r[:], cai[:], ALU.mult)
                b.tt(c3[:], L8i[:], car[:], ALU.mult)
                b.tt(c2_[:], c2_[:], c3[:], ALU.add)
                b.tt(Zsb[:, :, 0, 0], Zsb[:, :, 0, 0], c1[:], ALU.add)
                b.tt(Zsb[:, :, 1, 0], Zsb[:, :, 1, 0], c2_[:], ALU.add)
                b.tt(m1[:], Cc[:], Zr, ALU.mult)
                b.tt(m2[:], Sn[:], Zi, ALU.mult)
                b.tt(zr[:], m1[:], m2[:], ALU.add)
                b.tt(m1[:], Cc[:], Zi, ALU.mult)
                b.tt(m2[:], Sn[:], Zr, ALU.mult)
                b.tt(zi[:], m1[:], m2[:], ALU.subtract)
                fl = lambda a: a[:].rearrange("p q c -> p (q c)")
                b.scan(fl(m1), fl(R8m), fl(zr))
                b.scan(fl(m2), fl(R8m), fl(zi))
                zsr, zsi = m1, m2
                b.cp(Srb[:, :, 0:1], Srb[:, :, CH:CH + 1])
                b.cp(Sib[:, :, 0:1], Sib[:, :, CH:CH + 1])
                b.tt(zr[:], Cc[:], zsr[:], ALU.mult)
                b.tt(zi[:], Sn[:], zsi[:], ALU.mult)
                b.tt(Srb[:, :, 1:CH + 1], zr[:], zi[:], ALU.subtract)
                b.tt(zr[:], Sn[:], zsr[:], ALU.mult)
                b.tt(zi[:], Cc[:], zsi[:], ALU.mult)
                b.tt(Sib[:, :, 1:CH + 1], zr[:], zi[:], ALU.add)
                b.tt(c1[:], Cc[:, :, CH - 1], zsr[:, :, CH - 1], ALU.mult)
                b.tt(c2_[:], Sn[:, :, CH - 1], zsi[:, :, CH - 1], ALU.mult)
                b.tt(car[:], c1[:], c2_[:], ALU.subtract)
                b.tt(c1[:], Sn[:, :, CH - 1], zsr[:, :, CH - 1], ALU.mult)
                b.tt(c2_[:], Cc[:, :, CH - 1], zsi[:, :, CH - 1], ALU.mult)
                b.tt(cai[:], c1[:], c2_[:], ALU.add)
                for tt_ in range(4):
                    yv = p1[:, 0:BLK].rearrange("p (c i) -> p c i", i=8)
                    uv = uT[:, tt_, :].rearrange("p (c i) -> p c i", i=8)
                    ops = [dict(out=yv[:, :, d:8], lhsT=Kbd[:, d, tt_, :], rhs=uv[:, :, 0:8 - d],
                                start=(d == 0), stop=False) for d in range(8)]
                    for jp in range(4):
                        q = 4 * tt_ + jp
                        for i in range(8):
                            ops.append(dict(out=yv[32 * jp:32 * jp + 32, :, i], lhsT=OgR[:, i, q, :],
                                            rhs=Srb[:, q, 0:CH], start=False, stop=False, tp=(0, 32 * jp)))
                            ops.append(dict(out=yv[32 * jp:32 * jp + 32, :, i], lhsT=OgI[:, i, q, :],
                                            rhs=Sib[:, q, 0:CH], start=False, stop=(jp == 3 and i == 7),
                                            tp=(0, 32 * jp)))
                    b.mm(ops)
                    b.act(zT[:, tt_, :], p1[:, 0:BLK], AF.Gelu_apprx_tanh)
                for mt in range(4):
                    b.mm([dict(out=p2[:, 0:BLK], lhsT=Wglu[:, k, 128 * mt:128 * (mt + 1)], rhs=zT[:, k, :],
                               start=(k == 0), stop=(k == 3)) for k in range(4)])
                    b.act(sgl[:], p2[:, 0:BLK], AF.Sigmoid, bias=bglu[:, mt:mt + 1])
                    b.tt(mixT[:, 4 + mt, :], zT[:, mt, :], sgl[:], ALU.mult)
                for i in range(NTI):
                    cs = slice(128 * i, 128 * (i + 1))
                    for hf in range(2):
                        b.mm([dict(out=pO[:, 512 * hf:512 * (hf + 1)], lhsT=mixT[:, k, cs],
                                   rhs=Wout[:, k, 512 * hf:512 * (hf + 1)], start=(k == 0), stop=(k == 7))
                              for k in range(8)])
                    b.tt(xt[i][:], xt[i][:], pO[:], ALU.add)
                    t.dma('pool', x1[tok0 + 128 * i: tok0 + 128 * (i + 1), :], xt[i][:], f'x1st{i}')
            t.barrier()

        esB = contextlib.ExitStack()
        with esB:
            def SBb(name, shape, dt=F32):
                return SB(name, shape, dt, stack=esB)
            stg2 = [SBb("sg0", [128, 2048]), SBb("sg1", [128, 2048])]
            nw2 = SBb("nw2", [128, 8])
            t.dma('sp', nw2[:], norm_mlp_w.rearrange("(k p) -> p k", p=128), 'nw2', allow_slow_non_contiguous=True)
            Wup = SBb("Wup", [128, 8, 4096], BF16)
            Wdn = SBb("Wdn", [128, 32, 1024], BF16)
            for k2 in range(16):
                k, hf = k2 // 2, k2 % 2
                t.dma('sp', stg2[k2 % 2][:], w_up[128 * k:128 * (k + 1), 2048 * hf:2048 * (hf + 1)], f'sg{k2 % 2}')
                if k2 % 2 == 0:
                    b.act(Wup[:, k, 2048 * hf:2048 * (hf + 1)], stg2[0][:], AF.Copy, scale=nw2[:, k:k + 1])
                else:
                    b.ts(Wup[:, k, 2048 * hf:2048 * (hf + 1)], stg2[1][:], nw2[:, k:k + 1], ALU.mult)
            for k2 in range(16):
                t.dma('sp', stg2[k2 % 2][:].rearrange("p (a c) -> p a c", a=2),
                      w_down[256 * k2:256 * (k2 + 1), :].rearrange("(a p) c -> p a c", p=128), f'sg{k2 % 2}')
                src = stg2[k2 % 2][:].rearrange("p (a c) -> p a c", a=2)
                if k2 % 2 == 0:
                    b.act(Wdn[:, 2 * k2:2 * k2 + 2, :], src, AF.Copy)
                else:
                    b.cp(Wdn[:, 2 * k2:2 * k2 + 2, :], src)
            nfw = SBb("nfw", [128, 1024])
            t.dma('sp', nfw[:], norm_final_w.partition_broadcast(128), 'nfw')
            yt = [[SBb(f"yt{bu}_{i}", [128, 1024]) for i in range(2)] for bu in range(2)]
            sqb = SBb("sqb", [128, 1024], BF16)
            ssb = SBb("ssb", [128, 2]); rsb = SBb("rsb", [128, 2])
            hbb = SBb("hbb", [128, 1024], BF16)
            h2T = SBb("h2T", [128, 8, 256], BF16)
            rl = SBb("rl", [128, 256]);
            aT = SBb("aT", [128, 32, 256], BF16)
            x2 = SBb("x2", [128, 1024])
            ss2 = SBb("ss2", [128, 1]); rs2 = SBb("rs2", [128, 1])
            ot = [SBb("ot0", [128, 1024])]
            NB2 = NT // 256
            for blk in range(NB2):
                bu = blk % 2
                tok0 = blk * 256
                for i in range(2):
                    t.dma('sp', yt[bu][i][:], x1[tok0 + 128 * i: tok0 + 128 * (i + 1), :], f'yt{bu}_{i}')
                for i in range(2):
                    b.act(sqb[:], yt[bu][i][:], AF.Square, accum_out=ssb[:, i:i + 1])
                    b.act(rsb[:, i:i + 1], ssb[:, i:i + 1], AF.Sqrt, scale=1.0 / 1024, bias=epsc[:, 0:1])
                    b.recip(rsb[:, i:i + 1], rsb[:, i:i + 1])
                    b.act(hbb[:], yt[bu][i][:], AF.Copy, scale=rsb[:, i:i + 1])
                    b.tr([(pT[:, 128 * k:128 * (k + 1)], hbb[:, 128 * k:128 * (k + 1)]) for k in range(8)], identb[:])
                    b.cp(h2T[:, :, 128 * i:128 * (i + 1)], pT[:].rearrange("p (k c) -> p k c", c=128))
                pss = [p1, p2, p3, p4]
                for m in range(32):
                    ps = pss[m % 4]
                    b.mm([dict(out=ps[:, 0:256], lhsT=Wup[:, k, 128 * m:128 * (m + 1)], rhs=h2T[:, k, :],
                               start=(k == 0), stop=(k == 7)) for k in range(8)])
                    b.act(rl[:], ps[:, 0:256], AF.Relu)
                    b.tt(aT[:, m, :], rl[:], rl[:], ALU.mult)
                for i in range(2):
                    cs = slice(128 * i, 128 * (i + 1))
                    for hf in range(2):
                        b.mm([dict(out=pO[:, 512 * hf:512 * (hf + 1)], lhsT=aT[:, k, cs],
                                   rhs=Wdn[:, k, 512 * hf:512 * (hf + 1)], start=(k == 0), stop=(k == 31))
                              for k in range(32)])
                    b.tt(x2[:], yt[bu][i][:], pO[:], ALU.add)
                    b.act(sqb[:], x2[:], AF.Square, accum_out=ss2[:])
                    b.act(rs2[:], ss2[:], AF.Sqrt, scale=1.0 / 1024, bias=epsc[:, 0:1])
                    b.recip(rs2[:], rs2[:])
                    o_ = ot[0]
                    b.stt(o_[:], x2[:], rs2[:, 0:1], nfw[:], ALU.mult, ALU.mult)
                    t.dma('pool', out[tok0 + 128 * i: tok0 + 128 * (i + 1), :], o_[:], 'ost0')
            t.barrier()
    return nc


def _consts():
    j = np.arange(128)[:, None]
    i = np.arange(128)[None, :]
    cm = (j <= i).astype(np.float32)
    um = (j > i).astype(np.float32)
    rm = np.ones((128, 256), np.float32)
    rm[:, ::128] = 0.0
    rs = np.ones((128, 16, 32), np.float32)
    rs[:, :, 0] = 0.0
    return {
        "c_ident": np.eye(128, dtype=np.float32),
        "c_cmask": np.tile(cm, (1, 4)),
        "c_umask": um,
        "c_rmask": rm,
        "c_reset": rs.reshape(128, 512),
    }


_CACHE = {}


def kernel(**inputs):
    x = np.ascontiguousarray(inputs["x"], dtype=np.float32)
    Bn, S, Dm = x.shape
    ncores = NCORES if Bn % NCORES == 0 else 1
    nseq = Bn // ncores
    key = (nseq, S)
    if key not in _CACHE:
        _CACHE[key] = build(nseq, S)
    nc = _CACHE[key]
    shared = dict(_consts())
    for k, v in inputs.items():
        if k == "x":
            continue
        a = np.ascontiguousarray(v, dtype=np.float32)
        if k != "norm_final_w":
            a = a[0]
        shared[k] = np.ascontiguousarray(a)
    in_maps = []
    for c in range(ncores):
        m = dict(shared)
        m["x"] = np.ascontiguousarray(x[c * nseq:(c + 1) * nseq].reshape(nseq * S, Dm))
        in_maps.append(m)
    res = run_bass_kernel_spmd(nc, in_maps, core_ids=list(range(ncores)))
    outs = [np.asarray(r["out"]).reshape(nseq, S, Dm) for r in res.results]
    return np.concatenate(outs, axis=0).astype(np.float32)
```

```python
import contextlib
import math
import numpy as np
import concourse.bass as bass
import concourse.mybir as mybir
from concourse.bass_utils import run_bass_kernel_spmd

F32 = mybir.dt.float32
BF16 = mybir.dt.bfloat16
I32 = mybir.dt.int32
AF = mybir.ActivationFunctionType
ALU = mybir.AluOpType
EPS = 1e-6
NCORES = 8


class Trk:
    def __init__(s, nc, es):
        s.nc, s.es = nc, es
        s.E = dict(pe=nc.tensor, act=nc.scalar, dve=nc.vector, pool=nc.gpsimd, sp=nc.sync)
        s.cur = {}
        s.seen = {e: {} for e in s.E}
        s.res = {}
        s.dsem = {}
        s.nsem = 0
        s.allsems = []

    def _newsem(s, name):
        s.nsem += 1
        sem = s.es.enter_context(s.nc.semaphore(f"{name}{s.nsem}"))
        ent = [sem, 0, s.nsem]
        s.allsems.append(ent)
        return ent

    def _wait(s, e, tok):
        ent, val, weng = tok
        if weng == 'pe' and e == 'pe':
            return
        if s.seen[e].get(ent[2], 0) >= val:
            return
        s.E[e].wait_ge(ent[0], val)
        s.seen[e][ent[2]] = val

    def _deps(s, e, reads, writes):
        for k in reads:
            r = s.res.get(k)
            if r and r[0]:
                s._wait(e, r[0])
        for k in writes:
            r = s.res.get(k)
            if r:
                if r[0] and r[0][2] != e:
                    s._wait(e, r[0])
                for t in r[1].values():
                    if t[2] != e:
                        s._wait(e, t)

    def _commit(s, tok, reads, writes):
        for k in reads:
            r = s.res.setdefault(k, [None, {}])
            r[1][tok[0][2]] = tok
        for k in writes:
            s.res[k] = [tok, {}]

    def op(s, e, reads, writes, fn):
        s._deps(e, reads, writes)
        ins = fn()
        c = s.cur.get(e)
        if c is None or c[1] >= 30000:
            c = s._newsem(e)
            s.cur[e] = c
        c[1] += 1
        ins.then_inc(c[0], 1)
        s._commit((c, c[1], e), reads, writes)

    def dma(s, q, out, in_, key, reads=None, writes=None, **kw):
        reads = [in_.name] if reads is None else reads
        writes = [out.name] if writes is None else writes
        s._deps(q, reads, writes)
        d = s.dsem.get(key)
        if d is None:
            d = s._newsem("d")
            s.dsem[key] = d
        d[1] += 16
        s.E[q].dma_start(out=out, in_=in_, **kw).then_inc(d[0], 16)
        s._commit((d, d[1], 'dma'), reads, writes)

    def barrier(s, engines=('pe', 'act', 'dve', 'pool', 'sp')):
        for e in engines:
            for ent in s.allsems:
                if ent[1] > 0 and s.seen[e].get(ent[2], 0) < ent[1]:
                    s.E[e].wait_ge(ent[0], ent[1])
                    s.seen[e][ent[2]] = ent[1]


_SPLIT = {}


def _names(*aps):
    out = []
    for a in aps:
        if not hasattr(a, "name"):
            continue
        g = _SPLIT.get(a.name)
        if g is None:
            out.append(a.name)
            continue
        pstride = a.ap[0][0]
        col0 = a.offset % pstride
        ext = 1 + sum((cnt - 1) * abs(step) for step, cnt in a.ap[1:])
        for gi in range(col0 // g, (col0 + ext - 1) // g + 1):
            out.append(f"{a.name}:{gi}")
    return out


class HV:
    def __init__(s, t, off, w):
        s.t, s.off, s.w = t, off, w

    def __getitem__(s, key):
        if not isinstance(key, tuple):
            key = (key, slice(None))
        rows, cols = key
        c0 = cols.start or 0
        c1 = s.w if cols.stop is None else cols.stop
        return s.t[rows, s.off + c0:s.off + c1]


class B:
    def __init__(s, t):
        s.t = t
        s.nc = t.nc

    def act(s, out, in_, func, scale=None, bias=None, accum_out=None, eng='act'):
        kw = {}
        if scale is not None:
            kw['scale'] = scale
        if bias is not None:
            kw['bias'] = bias
        if accum_out is not None:
            kw['accum_out'] = accum_out
        s.t.op('act', _names(in_, scale, bias), _names(out, accum_out),
               lambda: s.nc.scalar.activation(out=out, in_=in_, func=func, **kw))

    def tt(s, out, a, b, op, e='dve'):
        s.t.op(e, _names(a, b), _names(out),
               lambda: s.t.E[e].tensor_tensor(out=out, in0=a, in1=b, op=op))

    def ts(s, out, a, s1, op0, s2=None, op1=None, e='dve'):
        kw = {} if op1 is None else {'op1': op1}
        s.t.op(e, _names(a, s1, s2), _names(out),
               lambda: s.t.E[e].tensor_scalar(out=out, in0=a, scalar1=s1, scalar2=s2, op0=op0, **kw))

    def stt(s, out, a, sc, b, op0, op1):
        s.t.op('dve', _names(a, sc, b), _names(out),
               lambda: s.nc.vector.scalar_tensor_tensor(out=out, in0=a, scalar=sc, in1=b, op0=op0, op1=op1))

    def cp(s, out, in_, e='dve'):
        s.t.op(e, _names(in_), _names(out), lambda: s.t.E[e].tensor_copy(out=out, in_=in_))

    def recip(s, out, in_):
        s.t.op('dve', _names(in_), _names(out), lambda: s.nc.vector.reciprocal(out=out, in_=in_))

    def memset(s, ap, v, e='dve'):
        s.t.op(e, [], _names(ap), lambda: s.t.E[e].memset(ap, v))

    def scan(s, out, d0, d1, init=0.0):
        s.t.op('dve', _names(d0, d1, init), _names(out),
               lambda: s.nc.vector.tensor_tensor_scan(out=out, data0=d0, data1=d1, initial=init,
                                                      op0=ALU.mult, op1=ALU.add))

    def mm(s, ops):
        reads, writes = [], []
        for o in ops:
            reads += _names(o['lhsT'], o['rhs'])
            writes += _names(o['out'])
        reads, writes = list(dict.fromkeys(reads)), list(dict.fromkeys(writes))
        n = len(ops)

        def fn():
            ins = None
            for o in ops:
                kw = {}
                if o.get('tp') is not None:
                    kw['tile_position'] = o['tp']
                ins = s.nc.tensor.matmul(o['out'], lhsT=o['lhsT'], rhs=o['rhs'],
                                         start=o['start'], stop=o['stop'], **kw)
            return ins
        s.t.op('pe', reads, writes, fn)

    def tr(s, outs_ins, ident):
        reads, writes = _names(ident), []
        for o, i in outs_ins:
            reads += _names(i)
            writes += _names(o)

        def fn():
            ins = None
            for o, i in outs_ins:
                ins = s.nc.tensor.transpose(o, i, ident)
            return ins
        s.t.op('pe', list(dict.fromkeys(reads)), list(dict.fromkeys(writes)), fn)


def build(NSEQ, S, dbg=False):
    nc = bass.Bass("TRN2", target_bir_lowering=False)
    NT = NSEQ * S
    BLK = 256
    NTI = BLK // 128
    CH = BLK // 8
    NB = S // BLK
    NBLK = NSEQ * NB

    def D(name, shape, kind="ExternalInput"):
        return nc.dram_tensor(name, shape, F32, kind=kind).ap()
    x = D("x", [NT, 1024])
    norm_mix_w = D("norm_mix_w", [1024])
    w_in = D("w_in", [1024, 2064])
    w_gk_up = D("w_gk_up", [16, 256])
    b_gk = D("b_gk", [256])
    gla_norm_w = D("gla_norm_w", [128])
    a_re = D("s5_a_re", [32, 64])
    a_im = D("s5_a_im", [32, 64])
    log_dt = D("s5_log_dt", [32])
    b_re = D("s5_b_re", [32, 64, 16])
    b_im = D("s5_b_im", [32, 64, 16])
    c_re = D("s5_c_re", [32, 16, 64])
    c_im = D("s5_c_im", [32, 16, 64])
    s5_d = D("s5_d", [32, 16])
    w_glu = D("w_glu", [512, 512])
    b_glu = D("b_glu", [512])
    w_out = D("w_out", [1024, 1024])
    norm_mlp_w = D("norm_mlp_w", [1024])
    w_up = D("w_mlp_up", [1024, 4096])
    w_down = D("w_mlp_down", [4096, 1024])
    norm_final_w = D("norm_final_w", [1024])
    c_ident = D("c_ident", [128, 128])
    c_cmask = D("c_cmask", [128, 512])
    c_umask = D("c_umask", [128, 128])
    c_rmask = D("c_rmask", [128, BLK])
    c_reset = D("c_reset", [128, 16 * CH])
    out = D("out", [NT, 1024], kind="ExternalOutput")
    x1 = D("x1s", [NT, 1024], kind="ExternalOutput" if dbg else "Internal")

    es = contextlib.ExitStack()
    with es:
        t = Trk(nc, es)
        b = B(t)

        def SB(name, shape, dt=F32, stack=es):
            return stack.enter_context(nc.sbuf_tensor(name, shape, dt))

        def PS(name, shape, dt=F32):
            return es.enter_context(nc.psum_tensor(name, shape, dt))

        pT = PS("pT", [128, 1024], BF16)
        pOa = PS("pOa", [128, 512])
        pOb = PS("pOb", [128, 512])
        p3 = PS("p3", [128, 512])
        pH1 = PS("pH1", [128, 512]); pH2 = PS("pH2", [128, 512])
        pH4 = PS("pH4", [128, 512]); pH5 = PS("pH5", [128, 512])
        h1a, h1b = HV(pH1, 0, 256), HV(pH1, 256, 256)
        h2a, h2b = HV(pH2, 0, 256), HV(pH2, 256, 256)
        h4a, h4b = HV(pH4, 0, 256), HV(pH4, 256, 256)
        h5a, h5b = HV(pH5, 0, 256), HV(pH5, 256, 256)
        p1, p2 = h1a, h1b

        ident = SB("ident", [128, 128])
        identb = SB("identb", [128, 128], BF16)
        t.dma('sp', ident[:], c_ident[:, :], 'ident')
        b.cp(identb[:], ident[:])
        epsc = SB("epsc", [128, 1])
        b.memset(epsc[:], EPS)

        esA = contextlib.ExitStack()
        with esA:
            def SA(name, shape, dt=F32):
                return SB(name, shape, dt, stack=esA)

            esS = contextlib.ExitStack()
            Kbd = SA("Kbd", [128, 8, 4, 128], BF16)
            ZW = SA("ZW", [128, 4, 8, 2, 2, 128], BF16)
            OgR = SA("OgR", [128, 8, 16, 32], BF16)
            OgI = SA("OgI", [128, 8, 16, 32], BF16)
            Cc = SA("Cc", [128, 16, CH])
            Sn = SA("Sn", [128, 16, CH])
            R8m = SA("R8m", [128, 16, CH])
            L8r = SA("L8r", [128, 16])
            L8i = SA("L8i", [128, 16])
            with esS:
                def SS(name, shape, dt=F32):
                    return SB(name, shape, dt, stack=esS)
                are = SS("are", [128, 16]); aim = SS("aim", [128, 16]); ldt = SS("ldt", [128, 16])
                for h in range(2):
                    t.dma('sp', are[64 * h:64 * h + 64, :], a_re.rearrange("(q h) n -> h n q", h=2)[h], f'are{h}',
                          allow_slow_non_contiguous=True)
                    t.dma('sp', aim[64 * h:64 * h + 64, :], a_im.rearrange("(q h) n -> h n q", h=2)[h], f'aim{h}',
                          allow_slow_non_contiguous=True)
                    t.dma('sp', ldt[64 * h:64 * h + 64, :],
                          log_dt.rearrange("(q h) -> h q", h=2)[h].partition_broadcast(64), f'ldt{h}',
                          allow_slow_non_contiguous=True)
                Br = SS("Br", [128, 16, 16]); Bi = SS("Bi", [128, 16, 16])
                for h in range(2):
                    t.dma('sp', Br[64 * h:64 * h + 64], b_re.rearrange("(q h) n p -> h n q p", h=2)[h], f'Br{h}')
                    t.dma('sp', Bi[64 * h:64 * h + 64], b_im.rearrange("(q h) n p -> h n q p", h=2)[h], f'Bi{h}')
                Cr = SS("Cr", [128, 16, 16]); Ci = SS("Ci", [128, 16, 16])
                Cnat = SS("Cnat", [128, 128])
                for (src, dst, nm) in ((c_re, Cr, 'cr'), (c_im, Ci, 'ci')):
                    for qb in range(2):
                        for ql in range(8):
                            t.dma('sp', Cnat[16 * ql:16 * ql + 16, :].rearrange("p (h n) -> p h n", h=2),
                                  src.rearrange("(q h) p n -> q p h n", h=2)[qb * 8 + ql], f'cn{ql}')
                        b.tr([(p1[:, 0:128], Cnat[:])], ident[:])
                        b.cp(dst[:, qb * 8:(qb + 1) * 8, :], p1[:, 0:128].rearrange("p (q c) -> p q c", c=16))
                dcol = SS("dcol", [128, 4])
                t.dma('sp', dcol[:], s5_d.rearrange("(t g) p -> (g p) t", t=4), 'dcol', allow_slow_non_contiguous=True)

                cnt = [0]

                def T(shape=(128, 16)):
                    cnt[0] += 1
                    return SS(f"tmp{cnt[0]}", list(shape))
                dtt = T(); adt = T(); th = T(); mag = T()
                b.act(dtt[:], ldt[:], AF.Exp)
                b.tt(adt[:], are[:], dtt[:], ALU.mult)
                b.tt(th[:], aim[:], dtt[:], ALU.mult)
                b.act(mag[:], adt[:], AF.Exp)
                r8 = T()
                b.act(r8[:], adt[:], AF.Exp, scale=8.0)
                kq = T(); ki = SS("ki", [128, 16], I32); kf = T(); rr = T()
                b.ts(kq[:], th[:], 1.0 / (2 * math.pi), ALU.mult)
                b.cp(ki[:], kq[:])
                b.cp(kf[:], ki[:])
                b.stt(rr[:], kf[:], -2 * math.pi, th[:], ALU.mult, ALU.add)
                s2 = T(); s4 = T(); c2 = T(); sinr = T(); cosr = T(); tq = T()
                b.act(s2[:], rr[:], AF.Sin, scale=0.5)
                b.act(s4[:], rr[:], AF.Sin, scale=0.25)
                b.tt(tq[:], s4[:], s4[:], ALU.mult)
                b.ts(c2[:], tq[:], -2.0, ALU.mult, 1.0, ALU.add)
                b.tt(sinr[:], s2[:], c2[:], ALU.mult)
                b.ts(sinr[:], sinr[:], 2.0, ALU.mult)
                b.tt(tq[:], s2[:], s2[:], ALU.mult)
                b.ts(cosr[:], tq[:], -2.0, ALU.mult, 1.0, ALU.add)
                LPr = SS("LPr", [128, 9, 16]); LPi = SS("LPi", [128, 9, 16])
                b.memset(LPr[:, 0, :], 1.0)
                b.memset(LPi[:, 0, :], 0.0)
                b.tt(LPr[:, 1, :], mag[:], cosr[:], ALU.mult)
                b.tt(LPi[:, 1, :], mag[:], sinr[:], ALU.mult)
                t1 = T(); t2 = T()

                def cmul(orr, oi, ar_, ai_, br_, bi_, ta, tb):
                    b.tt(ta, ar_, br_, ALU.mult)
                    b.tt(tb, ai_, bi_, ALU.mult)
                    b.tt(orr, ta, tb, ALU.subtract)
                    b.tt(ta, ar_, bi_, ALU.mult)
                    b.tt(tb, ai_, br_, ALU.mult)
                    b.tt(oi, ta, tb, ALU.add)
                for k in range(2, 9):
                    cmul(LPr[:, k, :], LPi[:, k, :], LPr[:, k - 1, :], LPi[:, k - 1, :], LPr[:, 1, :], LPi[:, 1, :],
                         t1[:], t2[:])
                b.cp(L8r[:], LPr[:, 8, :])
                b.cp(L8i[:], LPi[:, 8, :])
                den = T(); rden = T(); nr = T(); fr = T(); fi = T()
                b.tt(den[:], are[:], are[:], ALU.mult)
                b.tt(t1[:], aim[:], aim[:], ALU.mult)
                b.tt(den[:], den[:], t1[:], ALU.add)
                b.recip(rden[:], den[:])
                b.ts(nr[:], LPr[:, 1, :], -1.0, ALU.add)
                b.tt(t1[:], nr[:], are[:], ALU.mult)
                b.tt(t2[:], LPi[:, 1, :], aim[:], ALU.mult)
                b.tt(fr[:], t1[:], t2[:], ALU.add)
                b.tt(fr[:], fr[:], rden[:], ALU.mult)
                b.tt(t1[:], LPi[:, 1, :], are[:], ALU.mult)
                b.tt(t2[:], nr[:], aim[:], ALU.mult)
                b.tt(fi[:], t1[:], t2[:], ALU.subtract)
                b.tt(fi[:], fi[:], rden[:], ALU.mult)
                Bbr = SS("Bbr", [128, 16, 16]); Bbi = SS("Bbi", [128, 16, 16])
                u1 = SS("u1", [128, 16, 16]); u2 = SS("u2", [128, 16, 16])

                def bc(a, n=16):
                    return a.unsqueeze(2).to_broadcast([128, 16, n])
                cmul(Bbr[:], Bbi[:], bc(fr[:]), bc(fi[:]), Br[:], Bi[:], u1[:], u2[:])
                BPr = SS("BPr", [128, 16, 128]); BPi = SS("BPi", [128, 16, 128])
                CPr = SS("CPr", [128, 16, 128]); CPi = SS("CPi", [128, 16, 128])
                for (pad, src) in ((BPr, Bbr), (BPi, Bbi), (CPr, Cr), (CPi, Ci)):
                    b.memset(pad[:], 0.0)
                    for jp in range(4):
                        for h in range(2):
                            c0 = 32 * jp + 16 * h
                            b.cp(pad[64 * h:64 * h + 64, jp::4, c0:c0 + 16], src[64 * h:64 * h + 64, jp::4, :])
                GPr = SS("GPr", [128, 16, 128]); GPi = SS("GPi", [128, 16, 128])
                EPr = SS("EPr", [128, 16, 128]); EPi = SS("EPi", [128, 16, 128])
                v1 = SS("v1", [128, 16, 128]); v2 = SS("v2", [128, 16, 128])
                b.memset(Kbd[:], 0.0)
                for k in range(9):
                    lr, li = bc(LPr[:, k, :], 128), bc(LPi[:, k, :], 128)
                    cmul(EPr[:], EPi[:], CPr[:], CPi[:], lr, li, v1[:], v2[:])
                    if k >= 1:
                        for q in range(16):
                            c0 = 32 * (q % 4)
                            b.cp(OgR[:, k - 1, q, :], EPr[:, q, c0:c0 + 32], e='pool')
                            b.ts(OgI[:, k - 1, q, :], EPi[:, q, c0:c0 + 32], -1.0, ALU.mult, e='pool')
                    if k <= 7:
                        b.ts(v1[:], EPi[:], -1.0, ALU.mult)
                        for tt_ in range(4):
                            ops = []
                            for jp in range(4):
                                q = 4 * tt_ + jp
                                ops.append(dict(out=p2[:, 0:128], lhsT=BPr[:, q, :], rhs=EPr[:, q, :],
                                                start=(jp == 0), stop=False))
                                ops.append(dict(out=p2[:, 0:128], lhsT=BPi[:, q, :], rhs=v1[:, q, :],
                                                start=False, stop=(jp == 3)))
                            b.mm(ops)
                            if k == 0:
                                b.stt(Kbd[:, 0, tt_, :], ident[:], dcol[:, tt_:tt_ + 1], p2[:, 0:128],
                                      ALU.mult, ALU.add)
                            else:
                                b.cp(Kbd[:, k, tt_, :], p2[:, 0:128])
                        cmul(GPr[:], GPi[:], BPr[:], BPi[:], lr, li, v1[:], v2[:])
                        js = 7 - k
                        for tt_ in range(4):
                            for jj in range(2):
                                for ri, G in ((0, GPr), (1, GPi)):
                                    b.mm([dict(out=p3[:, 0:128], lhsT=G[:, 4 * tt_ + jj, :], rhs=ident[:],
                                               start=True, stop=False),
                                          dict(out=p3[:, 0:128], lhsT=G[:, 4 * tt_ + 2 + jj, :], rhs=ident[:],
                                               start=False, stop=True)])
                                    b.act(ZW[:, tt_, js, jj, ri, :], p3[:, 0:128], AF.Copy)
                p8r = T(); p8i = T(); q8r = T(); q8i = T()
                b.cp(p8r[:], cosr[:]); b.cp(p8i[:], sinr[:])
                for _ in range(3):
                    cmul(q8r[:], q8i[:], p8r[:], p8i[:], p8r[:], p8i[:], t1[:], t2[:])
                    b.cp(p8r[:], q8r[:]); b.cp(p8i[:], q8i[:])
                b.memset(Cc[:, :, 0:1], 1.0)
                b.memset(Sn[:, :, 0:1], 0.0)
                w1 = SS("w1", [128, 16, 32]); w2 = SS("w2", [128, 16, 32])
                for lv in range(int(math.log2(CH))):
                    m = 1 << lv
                    pr = p8r[:].unsqueeze(2).to_broadcast([128, 16, m])
                    pi_ = p8i[:].unsqueeze(2).to_broadcast([128, 16, m])
                    cmul(Cc[:, :, m:2 * m], Sn[:, :, m:2 * m], Cc[:, :, 0:m], Sn[:, :, 0:m], pr, pi_,
                         w1[:, :, 0:m], w2[:, :, 0:m])
                    cmul(q8r[:], q8i[:], p8r[:], p8i[:], p8r[:], p8i[:], t1[:], t2[:])
                    b.cp(p8r[:], q8r[:]); b.cp(p8i[:], q8i[:])
                rst = SS("rst", [128, 16, CH])
                t.dma('sp', rst[:], c_reset.rearrange("p (q c) -> p q c", c=CH), 'rst')
                b.tt(R8m[:], rst[:], r8[:].unsqueeze(2).to_broadcast([128, 16, CH]), ALU.mult)
                t.barrier()
            cmask = SA("cmask", [128, 512])
            umask = SA("umask", [128, 128])
            rmask = SA("rmask", [128, BLK])
            t.dma('sp', cmask[:], c_cmask[:, :], 'cmask')
            t.dma('sp', umask[:], c_umask[:, :], 'umask')
            t.dma('sp', rmask[:], c_rmask[:, :], 'rmask')

            nmw = SA("nmw", [128, 8])
            Win = SA("Win", [128, 8, 2064], BF16)
            Wout = SA("Wout", [128, 8, 1024], BF16)
            Wglu = SA("Wglu", [128, 4, 512], BF16)
            bglu = SA("bglu", [128, 4])
            WG = SA("WG", [17, 256])
            gnw = SA("gnw", [128, 512])
            gkl = SA("gkl", [17, BLK])
            esW = contextlib.ExitStack()
            with esW:
                stg = [SB("stg0", [128, 2064], F32, stack=esW), SB("stg1", [128, 2064], F32, stack=esW)]
                t.dma('sp', nmw[:], norm_mix_w.rearrange("(k p) -> p k", p=128), 'nmw',
                      allow_slow_non_contiguous=True)
                for k in range(8):
                    t.dma('sp', stg[k % 2][:], w_in[128 * k:128 * (k + 1), :], f'stg{k % 2}')
                    b.act(Win[:, k, :], stg[k % 2][:], AF.Copy, scale=nmw[:, k:k + 1])
                for k in range(8):
                    t.dma('sp', stg[k % 2][:, 0:1024], w_out[128 * k:128 * (k + 1), :], f'stg{k % 2}')
                    b.cp(Wout[:, k, :], stg[k % 2][:, 0:1024])
                for k in range(4):
                    t.dma('sp', stg[k % 2][:, 0:512], w_glu[128 * k:128 * (k + 1), :], f'stg{k % 2}')
                    b.cp(Wglu[:, k, :], stg[k % 2][:, 0:512])
                t.dma('sp', bglu[:], b_glu.rearrange("(m p) -> p m", p=128), 'bglu', allow_slow_non_contiguous=True)
                t.dma('sp', WG[0:16, :], w_gk_up[:, :], 'WG')
                t.dma('sp', WG[16:17, :], b_gk.rearrange("(o n) -> o n", o=1), 'WGb')
                for h in range(4):
                    t.dma('sp', gnw[:, 128 * h:128 * (h + 1)], gla_norm_w.partition_broadcast(128), f'gnw{h}')
                b.memset(gkl[:], 1.0)
                t.barrier()
            xa = [SA(f"xa{i}", [128, 1024]) for i in range(2)]
            xr = SA("xr", [128, 1024])
            ss = SA("ss", [128, 4]); rstd = SA("rstd", [128, 4])
            hb = SA("hb", [128, 1024], BF16)
            hT = [SA(f"hT{i}", [128, 8, BLK], BF16) for i in range(2)]
            gkl2 = [gkl, SA("gklb", [17, BLK])]
            b.memset(gkl2[1][:], 1.0)
            la = SA("la", [64, BLK]); cT = SA("cT", [64, BLK])
            eb = SA("eb", [64, BLK]); enb = SA("enb", [64, BLK])
            dec = [SA(f"dec{i}", [64, 4, NTI]) for i in range(2)]
            qe = [SA(f"qe{i}", [64, 4, BLK], BF16) for i in range(2)]
            ke = [SA(f"ke{i}", [64, 4, BLK], BF16) for i in range(2)]
            uT = [SA(f"uT{i}", [128, 4, BLK], BF16) for i in range(2)]
            vtm = SA("vtm", [128, 512], BF16)
            latm = SA("latm", [128, 256]); edt = SA("edt", [128, 256])
            kd = SA("kd", [128, 256], BF16)
            sg = SA("sg", [128, 512])
            scT = SA("scT", [128, 512], BF16)
            Sg = SA("Sg", [64, 4, 128]); Sgb = SA("Sgb", [64, 4, 128], BF16)
            ss4 = SA("ss4", [128, 4]); rs4 = SA("rs4", [128, 4]); sq2 = SA("sq2", [128, 128], BF16)
            mixtm = SA("mixtm", [128, 512], BF16)
            mixG = [SA(f"mixG{i}", [128, 4, BLK], BF16) for i in range(2)]
            mixS = [SA(f"mixS{i}", [128, 4, BLK], BF16) for i in range(2)]
            Zsb = SA("Zsb", [128, 16, 2, CH])
            zr = SA("zr", [128, 16, CH]); zi = SA("zi", [128, 16, CH])
            m1 = SA("m1", [128, 16, CH]); m2 = SA("m2", [128, 16, CH])
            Srb = [SA(f"Srb{i}", [128, 16, CH + 1], BF16) for i in range(2)]
            Sib = [SA(f"Sib{i}", [128, 16, CH + 1], BF16) for i in range(2)]
            car = SA("car", [128, 16]); cai = SA("cai", [128, 16])
            c1 = SA("c1", [128, 16]); c2_ = SA("c2", [128, 16]); c3 = SA("c3", [128, 16])
            zT = SA("zT", [128, 4, BLK], BF16)
            sgl = [SA(f"sgl{i}", [128, BLK]) for i in range(2)]

            def P1(blk):
                pb = blk % 2
                tok0 = blk * BLK
                hT_ = hT[pb]

                def proj_fm(ps, c0, M):
                    b.mm([dict(out=ps, lhsT=Win[:, k, c0:c0 + M], rhs=hT_[:, k, :], start=(k == 0), stop=(k == 7))
                          for k in range(8)])
                for i in range(NTI):
                    xt_ = xa[i % 2]
                    t.dma('sp', xt_[:], x[tok0 + 128 * i: tok0 + 128 * (i + 1), :], f'xa{i % 2}')
                    b.act(hb[:], xt_[:], AF.Square, accum_out=ss[:, i:i + 1])
                    b.act(rstd[:, i:i + 1], ss[:, i:i + 1], AF.Sqrt, scale=1.0 / 1024, bias=epsc[:, 0:1])
                    b.recip(rstd[:, i:i + 1], rstd[:, i:i + 1])
                    b.act(hb[:], xt_[:], AF.Copy, scale=rstd[:, i:i + 1])
                    yield
                    b.tr([(pT[:, 128 * k:128 * (k + 1)], hb[:, 128 * k:128 * (k + 1)]) for k in range(8)], identb[:])
                    b.cp(hT_[:, :, 128 * i:128 * (i + 1)], pT[:].rearrange("p (k c) -> p k c", c=128))
                    yield
                proj_fm(h1a[0:16, 0:BLK], 1024, 16)
                b.cp(gkl2[pb][0:16, :], h1a[0:16, 0:BLK])
                yield
                for tt_ in range(4):
                    ps = (h1a, h1b)[tt_ % 2]
                    proj_fm(ps[:, 0:BLK], 1552 + 128 * tt_, 128)
                    b.act(uT[pb][:, tt_, :], ps[:, 0:BLK], AF.Copy)
                    yield
                for h in range(4):
                    b.mm([dict(out=h1b[0:64, 0:BLK], lhsT=WG[:, 64 * h:64 * h + 64], rhs=gkl2[pb][:],
                               start=True, stop=True)])
                    b.act(la[:], h1b[0:64, 0:BLK], AF.Exp, scale=-1.0)
                    yield
                    b.act(la[:], la[:], AF.Ln, bias=1.0)
                    b.scan(cT[:], rmask[0:64, :], la[:])
                    b.act(eb[:], cT[:], AF.Exp, scale=-1.0 / 16)
                    b.act(enb[:], cT[:], AF.Exp, scale=1.0 / 16)
                    b.cp(dec[pb][:, h, :], eb[:, 127::128])
                    yield
                    proj_fm(h1a[0:64, 0:BLK], 64 * h, 64)
                    b.stt(qe[pb][:, h, :], h1a[0:64, 0:BLK], 0.125, eb[:], ALU.mult, ALU.mult)
                    yield
                    proj_fm(h1b[0:64, 0:BLK], 256 + 64 * h, 64)
                    b.tt(ke[pb][:, h, :], h1b[0:64, 0:BLK], enb[:], ALU.mult)
                    yield

            def P2(blk):
                pb = blk % 2
                hT_, qe_, ke_, gk_ = hT[pb], qe[pb], ke[pb], gkl2[pb]
                if blk % NB == 0:
                    b.memset(Sg[:], 0.0); b.memset(Sgb[:], 0.0)
                for i in range(NTI):
                    cs = slice(128 * i, 128 * (i + 1))

                    def proj_tm(ps, c0, N):
                        b.mm([dict(out=ps, lhsT=hT_[:, k, cs], rhs=Win[:, k, c0:c0 + N], start=(k == 0), stop=(k == 7))
                              for k in range(8)])
                    b.mm([dict(out=h4b[:, 0:256], lhsT=gk_[:, cs], rhs=WG[:], start=True, stop=True)])
                    b.act(latm[:], h4b[:, 0:256], AF.Exp, scale=-1.0)
                    yield
                    b.act(latm[:], latm[:], AF.Ln, bias=1.0)
                    b.mm([dict(out=h4b[:, 0:256], lhsT=umask[:], rhs=latm[:], start=True, stop=True)])
                    b.act(edt[:], h4b[:, 0:256], AF.Exp, scale=-1.0 / 16)
                    yield
                    proj_tm(h4a[:, 0:256], 256, 256)
                    b.tt(kd[:], h4a[:, 0:256], edt[:], ALU.mult)
                    yield
                    proj_tm(p3[:], 512, 512)
                    b.act(vtm[:], p3[:], AF.Copy)
                    yield
                    proj_tm(p3[:], 1040, 512)
                    b.act(sg[:], p3[:], AF.Silu)
                    b.tt(sg[:], sg[:], gnw[:], ALU.mult)
                    yield
                    b.mm([dict(out=pOb[:, 128 * h:128 * (h + 1)], lhsT=ke_[:, h, cs], rhs=qe_[:, h, cs],
                               start=True, stop=True) for h in range(4)])
                    b.tt(scT[:], pOb[:], cmask[:], ALU.mult)
                    yield
                    ops = []
                    for h in range(4):
                        hs = slice(128 * h, 128 * (h + 1))
                        ops.append(dict(out=pOa[:, hs], lhsT=scT[:, hs], rhs=vtm[:, hs], start=True, stop=False))
                        ops.append(dict(out=pOa[:, hs], lhsT=qe_[:, h, cs], rhs=Sgb[:, h, :], start=False, stop=True))
                    b.mm(ops)
                    upd = (h4a, h4b)
                    for hp in range(2):
                        b.mm([dict(out=upd[hp][0:64, 128 * (h % 2):128 * (h % 2 + 1)], lhsT=kd[:, 64 * h:64 * (h + 1)],
                                   rhs=vtm[:, 128 * h:128 * (h + 1)], start=True, stop=True)
                              for h in (2 * hp, 2 * hp + 1)])
                    for h in range(4):
                        b.stt(Sg[:, h, :], Sg[:, h, :], dec[pb][:, h, i:i + 1],
                              upd[h // 2][0:64, 128 * (h % 2):128 * (h % 2 + 1)], ALU.mult, ALU.add)
                    b.act(Sgb[:], Sg[:], AF.Copy)
                    for h in range(4):
                        b.act(sq2[:], pOa[:, 128 * h:128 * (h + 1)], AF.Square, accum_out=ss4[:, h:h + 1])
                    b.act(rs4[:], ss4[:], AF.Sqrt, scale=1.0 / 128, bias=epsc[:, 0:1])
                    b.recip(rs4[:], rs4[:])
                    for h in range(4):
                        hs = slice(128 * h, 128 * (h + 1))
                        b.stt(mixtm[:, hs], pOa[:, hs], rs4[:, h:h + 1], sg[:, hs], ALU.mult, ALU.mult)
                    yield
                    b.tr([(pT[:, 128 * h:128 * (h + 1)], mixtm[:, 128 * h:128 * (h + 1)]) for h in range(4)], identb[:])
                    b.cp(mixG[pb][:, :, cs], pT[:, 0:512].rearrange("p (k c) -> p k c", c=128))
                    yield

            def S5(blk):
                pb = blk % 2
                u_ = uT[pb]
                if blk % NB == 0:
                    b.memset(car[:], 0.0); b.memset(cai[:], 0.0)
                    b.memset(Srb[(blk + 1) % 2][:], 0.0); b.memset(Sib[(blk + 1) % 2][:], 0.0)
                zbanks = {0: pH5[:, 0:16 * CH], 1: pH2[:, 0:16 * CH]}
                for H in range(2):
                    zb = zbanks[H].rearrange("p (t j r c) -> p t j r c", t=4, j=2, r=2)
                    ops = []
                    for tt_ in range(4):
                        uv = u_[64 * H:64 * H + 64, tt_, :].rearrange("p (c i) -> p c i", i=8)
                        for jj in range(2):
                            for ri in range(2):
                                for js in range(8):
                                    ops.append(dict(out=zb[:, tt_, jj, ri, :],
                                                    lhsT=ZW[64 * H:64 * H + 64, tt_, js, jj, ri, :],
                                                    rhs=uv[:, :, js], start=(js == 0), stop=(js == 7)))
                    b.mm(ops)
                Z5 = Zsb[:].rearrange("p (t j) r c -> p t j r c", j=4)
                for H in range(2):
                    zb = zbanks[H].rearrange("p (t j r c) -> p t j r c", t=4, j=2, r=2)
                    for tt_ in range(4):
                        b.act(Z5[:, tt_, 2 * H:2 * H + 2, :, :], zb[:, tt_, :, :, :], AF.Copy)
                yield
                Zr, Zi = Zsb[:, :, 0, :], Zsb[:, :, 1, :]
                b.tt(c1[:], L8r[:], car[:], ALU.mult)
                b.tt(c2_[:], L8i[:], cai[:], ALU.mult)
                b.tt(c1[:], c1[:], c2_[:], ALU.subtract)
                b.tt(c2_[:], L8r[:], cai[:], ALU.mult)
                b.tt(c3[:], L8i[:], car[:], ALU.mult)
                b.tt(c2_[:], c2_[:], c3[:], ALU.add)
                b.tt(Zsb[:, :, 0, 0], Zsb[:, :, 0, 0], c1[:], ALU.add)
                b.tt(Zsb[:, :, 1, 0], Zsb[:, :, 1, 0], c2_[:], ALU.add)
                yield
                b.tt(m1[:], Cc[:], Zr, ALU.mult)
                b.tt(m2[:], Sn[:], Zi, ALU.mult)
                b.tt(zr[:], m1[:], m2[:], ALU.add)
                yield
                b.tt(m1[:], Cc[:], Zi, ALU.mult)
                b.tt(m2[:], Sn[:], Zr, ALU.mult)
                b.tt(zi[:], m1[:], m2[:], ALU.subtract)
                yield
                fl = lambda a_: a_[:].rearrange("p q c -> p (q c)")
                b.scan(fl(m1), fl(R8m), fl(zr))
                yield
                b.scan(fl(m2), fl(R8m), fl(zi))
                yield
                zsr, zsi = m1, m2
                So, Io = Srb[pb], Sib[pb]
                Sp, Ip = Srb[(blk + 1) % 2], Sib[(blk + 1) % 2]
                b.cp(So[:, :, 0:1], Sp[:, :, CH:CH + 1])
                b.cp(Io[:, :, 0:1], Ip[:, :, CH:CH + 1])
                b.tt(zr[:], Cc[:], zsr[:], ALU.mult)
                b.tt(zi[:], Sn[:], zsi[:], ALU.mult)
                b.tt(So[:, :, 1:CH + 1], zr[:], zi[:], ALU.subtract)
                yield
                b.tt(zr[:], Sn[:], zsr[:], ALU.mult)
                b.tt(zi[:], Cc[:], zsi[:], ALU.mult)
                b.tt(Io[:, :, 1:CH + 1], zr[:], zi[:], ALU.add)
                yield
                b.tt(c1[:], Cc[:, :, CH - 1], zsr[:, :, CH - 1], ALU.mult)
                b.tt(c2_[:], Sn[:, :, CH - 1], zsi[:, :, CH - 1], ALU.mult)
                b.tt(car[:], c1[:], c2_[:], ALU.subtract)
                b.tt(c1[:], Sn[:, :, CH - 1], zsr[:, :, CH - 1], ALU.mult)
                b.tt(c2_[:], Cc[:, :, CH - 1], zsi[:, :, CH - 1], ALU.mult)
                b.tt(cai[:], c1[:], c2_[:], ALU.add)
                yield
                for tt_ in range(4):
                    ps = (h5a, h5b)[tt_ % 2]
                    yv = ps[:, 0:BLK].rearrange("p (c i) -> p c i", i=8)
                    uv = u_[:, tt_, :].rearrange("p (c i) -> p c i", i=8)
                    ops = [dict(out=yv[:, :, d:8], lhsT=Kbd[:, d, tt_, :], rhs=uv[:, :, 0:8 - d],
                                start=(d == 0), stop=False) for d in range(8)]
                    for jp in range(4):
                        q = 4 * tt_ + jp
                        for i in range(8):
                            ops.append(dict(out=yv[32 * jp:32 * jp + 32, :, i], lhsT=OgR[:, i, q, :],
                                            rhs=So[:, q, 0:CH], start=False, stop=False, tp=(0, 32 * jp)))
                            ops.append(dict(out=yv[32 * jp:32 * jp + 32, :, i], lhsT=OgI[:, i, q, :],
                                            rhs=Io[:, q, 0:CH], start=False, stop=(jp == 3 and i == 7),
                                            tp=(0, 32 * jp)))
                    b.mm(ops)
                    b.act(zT[:, tt_, :], ps[:, 0:BLK], AF.Gelu_apprx_tanh)
                    yield
                for mt in range(4):
                    ps = (h2a, h2b)[mt % 2]
                    b.mm([dict(out=ps[:, 0:BLK], lhsT=Wglu[:, k, 128 * mt:128 * (mt + 1)], rhs=zT[:, k, :],
                               start=(k == 0), stop=(k == 3)) for k in range(4)])
                    b.act(sgl[mt % 2][:], ps[:, 0:BLK], AF.Sigmoid, bias=bglu[:, mt:mt + 1])
                    b.tt(mixS[pb][:, mt, :], zT[:, mt, :], sgl[mt % 2][:], ALU.mult)
                    yield

            def Rout(blk):
                pb = blk % 2
                tok0 = blk * BLK
                for i in range(NTI):
                    cs = slice(128 * i, 128 * (i + 1))
                    t.dma('sp', xr[:], x[tok0 + 128 * i: tok0 + 128 * (i + 1), :], 'xr')
                    for hf, po in ((0, pOa), (1, pOb)):
                        b.mm([dict(out=po[:], lhsT=(mixG[pb] if k < 4 else mixS[pb])[:, k % 4, cs],
                                   rhs=Wout[:, k, 512 * hf:512 * (hf + 1)], start=(k == 0), stop=(k == 7))
                              for k in range(8)])
                        b.tt(xr[:, 512 * hf:512 * (hf + 1)], xr[:, 512 * hf:512 * (hf + 1)], po[:], ALU.add)
                        yield
                    t.dma('pool', x1[tok0 + 128 * i: tok0 + 128 * (i + 1), :], xr[:], 'x1st')
                    yield

            def run_threads(gens_w):
                live = [[g, w] for g, w in gens_w if g is not None]
                while live:
                    for ent in list(live):
                        for _ in range(ent[1]):
                            try:
                                next(ent[0])
                            except StopIteration:
                                live.remove(ent)
                                break

            nblk_a = 0 if _DBG.get('skipA') else NBLK
            for k in range(nblk_a + 2 if nblk_a else 0):
                run_threads([
                    (P1(k) if k < nblk_a else None, 1),
                    (P2(k - 1) if 1 <= k <= nblk_a else None, 1),
                    (S5(k - 1) if 1 <= k <= nblk_a else None, 1),
                    (Rout(k - 2) if 2 <= k <= nblk_a + 1 else None, 1),
                ])
            t.barrier()

        esB = contextlib.ExitStack()
        with esB:
            def SBb(name, shape, dt=F32):
                return SB(name, shape, dt, stack=esB)
            BB = 512
            nw2 = SBb("nw2", [128, 8])
            t.dma('sp', nw2[:], norm_mlp_w.rearrange("(k p) -> p k", p=128), 'nw2', allow_slow_non_contiguous=True)
            Wup = SBb("Wup", [128, 8, 4096], BF16)
            Wdn = SBb("Wdn", [128, 32, 1024], BF16)
            nfw = SBb("nfw", [128, 1024])
            t.dma('sp', nfw[:], norm_final_w.partition_broadcast(128), 'nfw')
            esT = contextlib.ExitStack()
            with esT:
                stg2 = [SB("sg0", [128, 2048], F32, stack=esT), SB("sg1", [128, 2048], F32, stack=esT)]
                for k2 in range(16):
                    k, hf = k2 // 2, k2 % 2
                    t.dma('sp', stg2[k2 % 2][:], w_up[128 * k:128 * (k + 1), 2048 * hf:2048 * (hf + 1)], f'sg{k2 % 2}')
                    if k2 % 2 == 0:
                        b.act(Wup[:, k, 2048 * hf:2048 * (hf + 1)], stg2[0][:], AF.Copy, scale=nw2[:, k:k + 1])
                    else:
                        b.ts(Wup[:, k, 2048 * hf:2048 * (hf + 1)], stg2[1][:], nw2[:, k:k + 1], ALU.mult)
                for k2 in range(16):
                    t.dma('sp', stg2[k2 % 2][:].rearrange("p (a c) -> p a c", a=2),
                          w_down[256 * k2:256 * (k2 + 1), :].rearrange("(a p) c -> p a c", p=128), f'sg{k2 % 2}')
                    src = stg2[k2 % 2][:].rearrange("p (a c) -> p a c", a=2)
                    if k2 % 2 == 0:
                        b.act(Wdn[:, 2 * k2:2 * k2 + 2, :], src, AF.Copy)
                    else:
                        b.cp(Wdn[:, 2 * k2:2 * k2 + 2, :], src)
                t.barrier()
            ya = [SBb(f"ya{i}", [128, 1024]) for i in range(2)]
            yr = [SBb(f"yr{i}", [128, 1024]) for i in range(2)]
            ssb = SBb("ssb", [128, 4]); rsb = SBb("rsb", [128, 4])
            hbb = SBb("hbb", [128, 1024], BF16)
            h2T = [SBb(f"h2T{i}", [128, 8, BB], BF16) for i in range(1)]
            rl = [SBb(f"rl{i}", [128, BB]) for i in range(2)]
            aT = SBb("aT", [128, 32, BB], BF16)
            ss2 = SBb("ss2", [128, 2]); rs2 = SBb("rs2", [128, 2])
            NB2 = NT // BB
            pups = [pOa, pOb, p3]
            for blk in range(0 if _DBG.get('skipB') else NB2):
                tok0 = blk * BB
                for i in range(4):
                    y_ = ya[i % 2]
                    t.dma('sp', y_[:], x1[tok0 + 128 * i: tok0 + 128 * (i + 1), :], f'ya{i % 2}')
                    b.act(hbb[:], y_[:], AF.Square, accum_out=ssb[:, i:i + 1])
                    b.act(rsb[:, i:i + 1], ssb[:, i:i + 1], AF.Sqrt, scale=1.0 / 1024, bias=epsc[:, 0:1])
                    b.recip(rsb[:, i:i + 1], rsb[:, i:i + 1])
                    b.act(hbb[:], y_[:], AF.Copy, scale=rsb[:, i:i + 1])
                    b.tr([(pT[:, 128 * k:128 * (k + 1)], hbb[:, 128 * k:128 * (k + 1)]) for k in range(8)], identb[:])
                    b.cp(h2T[0][:, :, 128 * i:128 * (i + 1)], pT[:].rearrange("p (k c) -> p k c", c=128))
                for m in range(32):
                    ps = pups[m % 3]
                    b.mm([dict(out=ps[:], lhsT=Wup[:, k, 128 * m:128 * (m + 1)], rhs=h2T[0][:, k, :],
                               start=(k == 0), stop=(k == 7)) for k in range(8)])
                    b.act(rl[m % 2][:], ps[:], AF.Relu)
                    b.tt(aT[:, m, :], rl[m % 2][:], rl[m % 2][:], ALU.mult, e=('dve' if m % 4 != 3 else 'pool'))
                for i in range(4):
                    cs = slice(128 * i, 128 * (i + 1))
                    y_ = yr[i % 2]
                    t.dma('sp', y_[:], x1[tok0 + 128 * i: tok0 + 128 * (i + 1), :], f'yr{i % 2}')
                    pd = ((h1a, h1b), (h2a, h2b), (h4a, h4b), (h5a, h5b))
                    for qd in range(4):
                        po = pd[qd][i % 2]
                        b.mm([dict(out=po[:], lhsT=aT[:, k, cs], rhs=Wdn[:, k, 256 * qd:256 * (qd + 1)],
                                   start=(k == 0), stop=(k == 31)) for k in range(32)])
                        b.tt(y_[:, 256 * qd:256 * (qd + 1)], y_[:, 256 * qd:256 * (qd + 1)], po[:], ALU.add)
                    b.act(hbb[:], y_[:], AF.Square, accum_out=ss2[:, i % 2:i % 2 + 1])
                    b.act(rs2[:, i % 2:i % 2 + 1], ss2[:, i % 2:i % 2 + 1], AF.Sqrt, scale=1.0 / 1024, bias=epsc[:, 0:1])
                    b.recip(rs2[:, i % 2:i % 2 + 1], rs2[:, i % 2:i % 2 + 1])
                    b.stt(y_[:], y_[:], rs2[:, i % 2:i % 2 + 1], nfw[:], ALU.mult, ALU.mult)
                    t.dma('pool', out[tok0 + 128 * i: tok0 + 128 * (i + 1), :], y_[:], f'ost{i % 2}')
            t.barrier()
    return nc


def _consts():
    j = np.arange(128)[:, None]
    i = np.arange(128)[None, :]
    cm = (j <= i).astype(np.float32)
    um = (j > i).astype(np.float32)
    rm = np.ones((128, 256), np.float32)
    rm[:, ::128] = 0.0
    rs = np.ones((128, 16, 32), np.float32)
    rs[:, :, 0] = 0.0
    return {
        "c_ident": np.eye(128, dtype=np.float32),
        "c_cmask": np.tile(cm, (1, 4)),
        "c_umask": um,
        "c_rmask": rm,
        "c_reset": rs.reshape(128, 512),
    }


_CACHE = {}
_DBG = {}


def kernel(**inputs):
    x = np.ascontiguousarray(inputs["x"], dtype=np.float32)
    Bn, S, Dm = x.shape
    ncores = NCORES if Bn % NCORES == 0 else 1
    nseq = Bn // ncores
    key = (nseq, S)
    if key not in _CACHE:
        _CACHE[key] = build(nseq, S)
    nc = _CACHE[key]
    shared = dict(_consts())
    for k, v in inputs.items():
        if k == "x":
            continue
        a = np.ascontiguousarray(v, dtype=np.float32)
        if k != "norm_final_w":
            a = a[0]
        shared[k] = np.ascontiguousarray(a)
    in_maps = []
    for c in range(ncores):
        m = dict(shared)
        m["x"] = np.ascontiguousarray(x[c * nseq:(c + 1) * nseq].reshape(nseq * S, Dm))
        in_maps.append(m)
    res = run_bass_kernel_spmd(nc, in_maps, core_ids=list(range(ncores)))
    outs = [np.asarray(r["out"]).reshape(nseq, S, Dm) for r in res.results]
    return np.concatenate(outs, axis=0).astype(np.float32)
```

```python
import contextlib
import math
import numpy as np
import concourse.bass as bass
import concourse.mybir as mybir
from concourse.bass_utils import run_bass_kernel_spmd

F32 = mybir.dt.float32
BF16 = mybir.dt.bfloat16
I32 = mybir.dt.int32
AF = mybir.ActivationFunctionType
ALU = mybir.AluOpType
EPS = 1e-6
NCORES = 8
BLK_A = 256
L1_A = 4


class Trk:
    def __init__(s, nc, es):
        s.nc, s.es = nc, es
        s.E = dict(pe=nc.tensor, act=nc.scalar, dve=nc.vector, pool=nc.gpsimd, sp=nc.sync)
        s.cur = {}
        s.seen = {e: {} for e in s.E}
        s.res = {}
        s.dsem = {}
        s.nsem = 0
        s.allsems = []

    def _newsem(s, name):
        s.nsem += 1
        sem = s.es.enter_context(s.nc.semaphore(f"{name}{s.nsem}"))
        ent = [sem, 0, s.nsem]
        s.allsems.append(ent)
        return ent

    def _wait(s, e, tok):
        ent, val, weng = tok
        if weng == 'pe' and e == 'pe':
            return
        if s.seen[e].get(ent[2], 0) >= val:
            return
        s.E[e].wait_ge(ent[0], val)
        s.seen[e][ent[2]] = val

    def _deps(s, e, reads, writes):
        for k in reads:
            r = s.res.get(k)
            if r and r[0]:
                s._wait(e, r[0])
        for k in writes:
            r = s.res.get(k)
            if r:
                if r[0] and r[0][2] != e:
                    s._wait(e, r[0])
                for t in r[1].values():
                    if t[2] != e:
                        s._wait(e, t)

    def _commit(s, tok, reads, writes):
        for k in reads:
            r = s.res.setdefault(k, [None, {}])
            r[1][tok[0][2]] = tok
        for k in writes:
            s.res[k] = [tok, {}]

    def op(s, e, reads, writes, fn):
        s._deps(e, reads, writes)
        ins = fn()
        c = s.cur.get(e)
        if c is None or c[1] >= 30000:
            c = s._newsem(e)
            s.cur[e] = c
        c[1] += 1
        ins.then_inc(c[0], 1)
        s._commit((c, c[1], e), reads, writes)

    def dma(s, q, out, in_, key, reads=None, writes=None, **kw):
        reads = [in_.name] if reads is None else reads
        writes = [out.name] if writes is None else writes
        s._deps(q, reads, writes)
        d = s.dsem.get(key)
        if d is None:
            d = s._newsem("d")
            s.dsem[key] = d
        d[1] += 16
        s.E[q].dma_start(out=out, in_=in_, **kw).then_inc(d[0], 16)
        s._commit((d, d[1], 'dma'), reads, writes)

    def barrier(s, engines=('pe', 'act', 'dve', 'pool', 'sp')):
        for e in engines:
            for ent in s.allsems:
                if ent[1] > 0 and s.seen[e].get(ent[2], 0) < ent[1]:
                    s.E[e].wait_ge(ent[0], ent[1])
                    s.seen[e][ent[2]] = ent[1]


_SPLIT = {}


def _names(*aps):
    out = []
    for a in aps:
        if not hasattr(a, "name"):
            continue
        g = _SPLIT.get(a.name)
        if g is None:
            out.append(a.name)
            continue
        pstride = a.ap[0][0]
        col0 = a.offset % pstride
        ext = 1 + sum((cnt - 1) * abs(step) for step, cnt in a.ap[1:])
        for gi in range(col0 // g, (col0 + ext - 1) // g + 1):
            out.append(f"{a.name}:{gi}")
    return out


class HV:
    def __init__(s, t, off, w):
        s.t, s.off, s.w = t, off, w

    def __getitem__(s, key):
        if not isinstance(key, tuple):
            key = (key, slice(None))
        rows, cols = key
        c0 = cols.start or 0
        c1 = s.w if cols.stop is None else cols.stop
        return s.t[rows, s.off + c0:s.off + c1]


class B:
    def __init__(s, t):
        s.t = t
        s.nc = t.nc

    def act(s, out, in_, func, scale=None, bias=None, accum_out=None, eng='act'):
        kw = {}
        if scale is not None:
            kw['scale'] = scale
        if bias is not None:
            kw['bias'] = bias
        if accum_out is not None:
            kw['accum_out'] = accum_out
        s.t.op('act', _names(in_, scale, bias), _names(out, accum_out),
               lambda: s.nc.scalar.activation(out=out, in_=in_, func=func, **kw))

    def tt(s, out, a, b, op, e='dve'):
        s.t.op(e, _names(a, b), _names(out),
               lambda: s.t.E[e].tensor_tensor(out=out, in0=a, in1=b, op=op))

    def ts(s, out, a, s1, op0, s2=None, op1=None, e='dve'):
        kw = {} if op1 is None else {'op1': op1}
        s.t.op(e, _names(a, s1, s2), _names(out),
               lambda: s.t.E[e].tensor_scalar(out=out, in0=a, scalar1=s1, scalar2=s2, op0=op0, **kw))

    def stt(s, out, a, sc, b, op0, op1):
        s.t.op('dve', _names(a, sc, b), _names(out),
               lambda: s.nc.vector.scalar_tensor_tensor(out=out, in0=a, scalar=sc, in1=b, op0=op0, op1=op1))

    def cp(s, out, in_, e='dve'):
        s.t.op(e, _names(in_), _names(out), lambda: s.t.E[e].tensor_copy(out=out, in_=in_))

    def recip(s, out, in_):
        s.t.op('dve', _names(in_), _names(out), lambda: s.nc.vector.reciprocal(out=out, in_=in_))

    def memset(s, ap, v, e='dve'):
        s.t.op(e, [], _names(ap), lambda: s.t.E[e].memset(ap, v))

    def scan(s, out, d0, d1, init=0.0):
        s.t.op('dve', _names(d0, d1, init), _names(out),
               lambda: s.nc.vector.tensor_tensor_scan(out=out, data0=d0, data1=d1, initial=init,
                                                      op0=ALU.mult, op1=ALU.add))

    def mm(s, ops):
        reads, writes = [], []
        for o in ops:
            reads += _names(o['lhsT'], o['rhs'])
            writes += _names(o['out'])
        reads, writes = list(dict.fromkeys(reads)), list(dict.fromkeys(writes))
        n = len(ops)

        def fn():
            ins = None
            for o in ops:
                kw = {}
                if o.get('tp') is not None:
                    kw['tile_position'] = o['tp']
                ins = s.nc.tensor.matmul(o['out'], lhsT=o['lhsT'], rhs=o['rhs'],
                                         start=o['start'], stop=o['stop'], **kw)
            return ins
        s.t.op('pe', reads, writes, fn)

    def tr(s, outs_ins, ident):
        reads, writes = _names(ident), []
        for o, i in outs_ins:
            reads += _names(i)
            writes += _names(o)

        def fn():
            ins = None
            for o, i in outs_ins:
                ins = s.nc.tensor.transpose(o, i, ident)
            return ins
        s.t.op('pe', list(dict.fromkeys(reads)), list(dict.fromkeys(writes)), fn)


def build(NSEQ, S, dbg=False):
    nc = bass.Bass("TRN2", target_bir_lowering=False)
    NT = NSEQ * S
    BLK = BLK_A
    NTI = BLK // 128
    L1 = L1_A
    CH = BLK // L1
    NB = S // BLK
    NBLK = NSEQ * NB

    def D(name, shape, kind="ExternalInput"):
        return nc.dram_tensor(name, shape, F32, kind=kind).ap()
    x = D("x", [NT, 1024])
    norm_mix_w = D("norm_mix_w", [1024])
    w_in = D("w_in", [1024, 2064])
    w_gk_up = D("w_gk_up", [16, 256])
    b_gk = D("b_gk", [256])
    gla_norm_w = D("gla_norm_w", [128])
    a_re = D("s5_a_re", [32, 64])
    a_im = D("s5_a_im", [32, 64])
    log_dt = D("s5_log_dt", [32])
    b_re = D("s5_b_re", [32, 64, 16])
    b_im = D("s5_b_im", [32, 64, 16])
    c_re = D("s5_c_re", [32, 16, 64])
    c_im = D("s5_c_im", [32, 16, 64])
    s5_d = D("s5_d", [32, 16])
    w_glu = D("w_glu", [512, 512])
    b_glu = D("b_glu", [512])
    w_out = D("w_out", [1024, 1024])
    norm_mlp_w = D("norm_mlp_w", [1024])
    w_up = D("w_mlp_up", [1024, 4096])
    w_down = D("w_mlp_down", [4096, 1024])
    norm_final_w = D("norm_final_w", [1024])
    c_ident = D("c_ident", [128, 128])
    c_cmask = D("c_cmask", [128, 512])
    c_umask = D("c_umask", [128, 128])
    c_rmask = D("c_rmask", [128, BLK])
    c_reset = D("c_reset", [128, 16 * CH])
    out = D("out", [NT, 1024], kind="ExternalOutput")
    x1 = D("x1s", [NT, 1024], kind="ExternalOutput" if dbg else "Internal")

    es = contextlib.ExitStack()
    with es:
        t = Trk(nc, es)
        b = B(t)

        def SB(name, shape, dt=F32, stack=es):
            return stack.enter_context(nc.sbuf_tensor(name, shape, dt))

        def PS(name, shape, dt=F32):
            return es.enter_context(nc.psum_tensor(name, shape, dt))

        pT = PS("pT", [128, 1024], BF16)
        pOa = PS("pOa", [128, 512])
        pOb = PS("pOb", [128, 512])
        p3 = PS("p3", [128, 512])
        pH1 = PS("pH1", [128, 512]); pH2 = PS("pH2", [128, 512])
        pH4 = PS("pH4", [128, 512]); pH5 = PS("pH5", [128, 512])
        h1a, h1b = HV(pH1, 0, 256), HV(pH1, 256, 256)
        h2a, h2b = HV(pH2, 0, 256), HV(pH2, 256, 256)
        h4a, h4b = HV(pH4, 0, 256), HV(pH4, 256, 256)
        h5a, h5b = HV(pH5, 0, 256), HV(pH5, 256, 256)
        p1, p2 = h1a, h1b

        ident = SB("ident", [128, 128])
        identb = SB("identb", [128, 128], BF16)
        t.dma('sp', ident[:], c_ident[:, :], 'ident')
        b.cp(identb[:], ident[:])
        epsc = SB("epsc", [128, 1])
        b.memset(epsc[:], EPS)

        esA = contextlib.ExitStack()
        with esA:
            def SA(name, shape, dt=F32):
                return SB(name, shape, dt, stack=esA)

            esS = contextlib.ExitStack()
            Kbd = SA("Kbd", [128, L1, 4, 128], BF16)
            ZW = SA("ZW", [128, 4, L1, 2, 2, 128], BF16)
            OgR = SA("OgR", [128, L1, 16, 32], BF16)
            OgI = SA("OgI", [128, L1, 16, 32], BF16)
            Cc = SA("Cc", [128, 16, CH])
            Sn = SA("Sn", [128, 16, CH])
            R8m = SA("R8m", [128, 16, CH])
            L8r = SA("L8r", [128, 16])
            L8i = SA("L8i", [128, 16])
            with esS:
                def SS(name, shape, dt=F32):
                    return SB(name, shape, dt, stack=esS)
                are = SS("are", [128, 16]); aim = SS("aim", [128, 16]); ldt = SS("ldt", [128, 16])
                for h in range(2):
                    t.dma('sp', are[64 * h:64 * h + 64, :], a_re.rearrange("(q h) n -> h n q", h=2)[h], f'are{h}',
                          allow_slow_non_contiguous=True)
                    t.dma('sp', aim[64 * h:64 * h + 64, :], a_im.rearrange("(q h) n -> h n q", h=2)[h], f'aim{h}',
                          allow_slow_non_contiguous=True)
                    t.dma('sp', ldt[64 * h:64 * h + 64, :],
                          log_dt.rearrange("(q h) -> h q", h=2)[h].partition_broadcast(64), f'ldt{h}',
                          allow_slow_non_contiguous=True)
                Br = SS("Br", [128, 16, 16]); Bi = SS("Bi", [128, 16, 16])
                for h in range(2):
                    t.dma('sp', Br[64 * h:64 * h + 64], b_re.rearrange("(q h) n p -> h n q p", h=2)[h], f'Br{h}')
                    t.dma('sp', Bi[64 * h:64 * h + 64], b_im.rearrange("(q h) n p -> h n q p", h=2)[h], f'Bi{h}')
                Cr = SS("Cr", [128, 16, 16]); Ci = SS("Ci", [128, 16, 16])
                Cnat = SS("Cnat", [128, 128])
                for (src, dst, nm) in ((c_re, Cr, 'cr'), (c_im, Ci, 'ci')):
                    for qb in range(2):
                        for ql in range(8):
                            t.dma('sp', Cnat[16 * ql:16 * ql + 16, :].rearrange("p (h n) -> p h n", h=2),
                                  src.rearrange("(q h) p n -> q p h n", h=2)[qb * 8 + ql], f'cn{ql}')
                        b.tr([(p1[:, 0:128], Cnat[:])], ident[:])
                        b.cp(dst[:, qb * 8:(qb + 1) * 8, :], p1[:, 0:128].rearrange("p (q c) -> p q c", c=16))
                dcol = SS("dcol", [128, 4])
                t.dma('sp', dcol[:], s5_d.rearrange("(t g) p -> (g p) t", t=4), 'dcol', allow_slow_non_contiguous=True)

                cnt = [0]

                def T(shape=(128, 16)):
                    cnt[0] += 1
                    return SS(f"tmp{cnt[0]}", list(shape))
                dtt = T(); adt = T(); th = T(); mag = T()
                b.act(dtt[:], ldt[:], AF.Exp)
                b.tt(adt[:], are[:], dtt[:], ALU.mult)
                b.tt(th[:], aim[:], dtt[:], ALU.mult)
                b.act(mag[:], adt[:], AF.Exp)
                r8 = T()
                b.act(r8[:], adt[:], AF.Exp, scale=float(L1))
                kq = T(); ki = SS("ki", [128, 16], I32); kf = T(); rr = T()
                b.ts(kq[:], th[:], 1.0 / (2 * math.pi), ALU.mult)
                b.cp(ki[:], kq[:])
                b.cp(kf[:], ki[:])
                b.stt(rr[:], kf[:], -2 * math.pi, th[:], ALU.mult, ALU.add)
                s2 = T(); s4 = T(); c2 = T(); sinr = T(); cosr = T(); tq = T()
                b.act(s2[:], rr[:], AF.Sin, scale=0.5)
                b.act(s4[:], rr[:], AF.Sin, scale=0.25)
                b.tt(tq[:], s4[:], s4[:], ALU.mult)
                b.ts(c2[:], tq[:], -2.0, ALU.mult, 1.0, ALU.add)
                b.tt(sinr[:], s2[:], c2[:], ALU.mult)
                b.ts(sinr[:], sinr[:], 2.0, ALU.mult)
                b.tt(tq[:], s2[:], s2[:], ALU.mult)
                b.ts(cosr[:], tq[:], -2.0, ALU.mult, 1.0, ALU.add)
                LPr = SS("LPr", [128, L1 + 1, 16]); LPi = SS("LPi", [128, L1 + 1, 16])
                b.memset(LPr[:, 0, :], 1.0)
                b.memset(LPi[:, 0, :], 0.0)
                b.tt(LPr[:, 1, :], mag[:], cosr[:], ALU.mult)
                b.tt(LPi[:, 1, :], mag[:], sinr[:], ALU.mult)
                t1 = T(); t2 = T()

                def cmul(orr, oi, ar_, ai_, br_, bi_, ta, tb):
                    b.tt(ta, ar_, br_, ALU.mult)
                    b.tt(tb, ai_, bi_, ALU.mult)
                    b.tt(orr, ta, tb, ALU.subtract)
                    b.tt(ta, ar_, bi_, ALU.mult)
                    b.tt(tb, ai_, br_, ALU.mult)
                    b.tt(oi, ta, tb, ALU.add)
                for k in range(2, L1 + 1):
                    cmul(LPr[:, k, :], LPi[:, k, :], LPr[:, k - 1, :], LPi[:, k - 1, :], LPr[:, 1, :], LPi[:, 1, :],
                         t1[:], t2[:])
                b.cp(L8r[:], LPr[:, L1, :])
                b.cp(L8i[:], LPi[:, L1, :])
                den = T(); rden = T(); nr = T(); fr = T(); fi = T()
                b.tt(den[:], are[:], are[:], ALU.mult)
                b.tt(t1[:], aim[:], aim[:], ALU.mult)
                b.tt(den[:], den[:], t1[:], ALU.add)
                b.recip(rden[:], den[:])
                b.ts(nr[:], LPr[:, 1, :], -1.0, ALU.add)
                b.tt(t1[:], nr[:], are[:], ALU.mult)
                b.tt(t2[:], LPi[:, 1, :], aim[:], ALU.mult)
                b.tt(fr[:], t1[:], t2[:], ALU.add)
                b.tt(fr[:], fr[:], rden[:], ALU.mult)
                b.tt(t1[:], LPi[:, 1, :], are[:], ALU.mult)
                b.tt(t2[:], nr[:], aim[:], ALU.mult)
                b.tt(fi[:], t1[:], t2[:], ALU.subtract)
                b.tt(fi[:], fi[:], rden[:], ALU.mult)
                Bbr = SS("Bbr", [128, 16, 16]); Bbi = SS("Bbi", [128, 16, 16])
                u1 = SS("u1", [128, 16, 16]); u2 = SS("u2", [128, 16, 16])

                def bc(a, n=16):
                    return a.unsqueeze(2).to_broadcast([128, 16, n])
                cmul(Bbr[:], Bbi[:], bc(fr[:]), bc(fi[:]), Br[:], Bi[:], u1[:], u2[:])
                BPr = SS("BPr", [128, 16, 128]); BPi = SS("BPi", [128, 16, 128])
                CPr = SS("CPr", [128, 16, 128]); CPi = SS("CPi", [128, 16, 128])
                for (pad, src) in ((BPr, Bbr), (BPi, Bbi), (CPr, Cr), (CPi, Ci)):
                    b.memset(pad[:], 0.0)
                    for jp in range(4):
                        for h in range(2):
                            c0 = 32 * jp + 16 * h
                            b.cp(pad[64 * h:64 * h + 64, jp::4, c0:c0 + 16], src[64 * h:64 * h + 64, jp::4, :])
                GPr = SS("GPr", [128, 16, 128]); GPi = SS("GPi", [128, 16, 128])
                EPr = SS("EPr", [128, 16, 128]); EPi = SS("EPi", [128, 16, 128])
                v1 = SS("v1", [128, 16, 128]); v2 = SS("v2", [128, 16, 128])
                b.memset(Kbd[:], 0.0)
                for k in range(L1 + 1):
                    lr, li = bc(LPr[:, k, :], 128), bc(LPi[:, k, :], 128)
                    cmul(EPr[:], EPi[:], CPr[:], CPi[:], lr, li, v1[:], v2[:])
                    if k >= 1:
                        for jp in range(4):
                            c0 = 32 * jp
                            b.cp(OgR[:, k - 1, jp::4, :], EPr[:, jp::4, c0:c0 + 32], e='pool')
                            b.ts(OgI[:, k - 1, jp::4, :], EPi[:, jp::4, c0:c0 + 32], -1.0, ALU.mult, e='pool')
                    if k <= L1 - 1:
                        b.ts(v1[:], EPi[:], -1.0, ALU.mult)
                        for tt_ in range(4):
                            ops = []
                            for jp in range(4):
                                q = 4 * tt_ + jp
                                ops.append(dict(out=p2[:, 0:128], lhsT=BPr[:, q, :], rhs=EPr[:, q, :],
                                                start=(jp == 0), stop=False))
                                ops.append(dict(out=p2[:, 0:128], lhsT=BPi[:, q, :], rhs=v1[:, q, :],
                                                start=False, stop=(jp == 3)))
                            b.mm(ops)
                            if k == 0:
                                b.stt(Kbd[:, 0, tt_, :], ident[:], dcol[:, tt_:tt_ + 1], p2[:, 0:128],
                                      ALU.mult, ALU.add)
                            else:
                                b.cp(Kbd[:, k, tt_, :], p2[:, 0:128])
                        cmul(GPr[:], GPi[:], BPr[:], BPi[:], lr, li, v1[:], v2[:])
                        js = L1 - 1 - k
                        for tt_ in range(4):
                            for jj in range(2):
                                for ri, G in ((0, GPr), (1, GPi)):
                                    b.mm([dict(out=p3[:, 0:128], lhsT=G[:, 4 * tt_ + jj, :], rhs=ident[:],
                                               start=True, stop=False),
                                          dict(out=p3[:, 0:128], lhsT=G[:, 4 * tt_ + 2 + jj, :], rhs=ident[:],
                                               start=False, stop=True)])
                                    b.act(ZW[:, tt_, js, jj, ri, :], p3[:, 0:128], AF.Copy)
                p8r = T(); p8i = T(); q8r = T(); q8i = T()
                b.cp(p8r[:], cosr[:]); b.cp(p8i[:], sinr[:])
                for _ in range(int(math.log2(L1))):
                    cmul(q8r[:], q8i[:], p8r[:], p8i[:], p8r[:], p8i[:], t1[:], t2[:])
                    b.cp(p8r[:], q8r[:]); b.cp(p8i[:], q8i[:])
                b.memset(Cc[:, :, 0:1], 1.0)
                b.memset(Sn[:, :, 0:1], 0.0)
                w1 = SS("w1", [128, 16, 32]); w2 = SS("w2", [128, 16, 32])
                for lv in range(int(math.log2(CH))):
                    m = 1 << lv
                    pr = p8r[:].unsqueeze(2).to_broadcast([128, 16, m])
                    pi_ = p8i[:].unsqueeze(2).to_broadcast([128, 16, m])
                    cmul(Cc[:, :, m:2 * m], Sn[:, :, m:2 * m], Cc[:, :, 0:m], Sn[:, :, 0:m], pr, pi_,
                         w1[:, :, 0:m], w2[:, :, 0:m])
                    cmul(q8r[:], q8i[:], p8r[:], p8i[:], p8r[:], p8i[:], t1[:], t2[:])
                    b.cp(p8r[:], q8r[:]); b.cp(p8i[:], q8i[:])
                rst = SS("rst", [128, 16, CH])
                t.dma('sp', rst[:], c_reset.rearrange("p (q c) -> p q c", c=CH), 'rst')
                b.tt(R8m[:], rst[:], r8[:].unsqueeze(2).to_broadcast([128, 16, CH]), ALU.mult)
                t.barrier()
            cmask = SA("cmask", [128, 512])
            umask = SA("umask", [128, 128])
            rmask = SA("rmask", [128, BLK])
            t.dma('sp', cmask[:], c_cmask[:, :], 'cmask')
            t.dma('sp', umask[:], c_umask[:, :], 'umask')
            t.dma('sp', rmask[:], c_rmask[:, :], 'rmask')

            nmw = SA("nmw", [128, 8])
            Win = SA("Win", [128, 8, 2064], BF16)
            Wout = SA("Wout", [128, 8, 1024], BF16)
            Wglu = SA("Wglu", [128, 4, 512], BF16)
            bglu = SA("bglu", [128, 4])
            WG = SA("WG", [17, 256])
            gnw = SA("gnw", [128, 512])
            gkl = SA("gkl", [17, BLK])
            esW = contextlib.ExitStack()
            with esW:
                stg = [SB("stg0", [128, 2064], F32, stack=esW), SB("stg1", [128, 2064], F32, stack=esW)]
                t.dma('sp', nmw[:], norm_mix_w.rearrange("(k p) -> p k", p=128), 'nmw',
                      allow_slow_non_contiguous=True)
                for k in range(8):
                    t.dma('sp', stg[k % 2][:], w_in[128 * k:128 * (k + 1), :], f'stg{k % 2}')
                    b.act(Win[:, k, :], stg[k % 2][:], AF.Copy, scale=nmw[:, k:k + 1])
                for k in range(8):
                    t.dma('sp', stg[k % 2][:, 0:1024], w_out[128 * k:128 * (k + 1), :], f'stg{k % 2}')
                    b.cp(Wout[:, k, :], stg[k % 2][:, 0:1024])
                for k in range(4):
                    t.dma('sp', stg[k % 2][:, 0:512], w_glu[128 * k:128 * (k + 1), :], f'stg{k % 2}')
                    b.cp(Wglu[:, k, :], stg[k % 2][:, 0:512])
                t.dma('sp', bglu[:], b_glu.rearrange("(m p) -> p m", p=128), 'bglu', allow_slow_non_contiguous=True)
                t.dma('sp', WG[0:16, :], w_gk_up[:, :], 'WG')
                t.dma('sp', WG[16:17, :], b_gk.rearrange("(o n) -> o n", o=1), 'WGb')
                for h in range(4):
                    t.dma('sp', gnw[:, 128 * h:128 * (h + 1)], gla_norm_w.partition_broadcast(128), f'gnw{h}')
                b.memset(gkl[:], 1.0)
                t.barrier()
            xa = [SA(f"xa{i}", [128, 1024]) for i in range(2)]
            xr = SA("xr", [128, 1024])
            ss = SA("ss", [128, 4]); rstd = SA("rstd", [128, 4])
            hb = SA("hb", [128, 1024], BF16)
            hT = [SA(f"hT{i}", [128, 8, BLK], BF16) for i in range(2)]
            gkl2 = [gkl, SA("gklb", [17, BLK])]
            b.memset(gkl2[1][:], 1.0)
            la = SA("la", [64, BLK]); cT = SA("cT", [64, BLK])
            eb = SA("eb", [64, BLK]); enb = SA("enb", [64, BLK])
            dec = [SA(f"dec{i}", [64, 4, NTI]) for i in range(2)]
            qe = [SA(f"qe{i}", [64, 4, BLK], BF16) for i in range(2)]
            ke = [SA(f"ke{i}", [64, 4, BLK], BF16) for i in range(2)]
            uT = [SA(f"uT{i}", [128, 4, BLK], BF16) for i in range(2)]
            vtm = SA("vtm", [128, 512], BF16)
            latm = SA("latm", [128, 256]); edt = SA("edt", [128, 256])
            kd = SA("kd", [128, 256], BF16)
            sg = SA("sg", [128, 512])
            scT = SA("scT", [128, 512], BF16)
            Sg = SA("Sg", [64, 4, 128]); Sgb = SA("Sgb", [64, 4, 128], BF16)
            ss4 = SA("ss4", [128, 4]); rs4 = SA("rs4", [128, 4]); sq2 = SA("sq2", [128, 128], BF16)
            mixtm = SA("mixtm", [128, 512], BF16)
            mixG = [SA(f"mixG{i}", [128, 4, BLK], BF16) for i in range(2)]
            mixS = [SA(f"mixS{i}", [128, 4, BLK], BF16) for i in range(2)]
            Zsb = SA("Zsb", [128, 16, 2, CH])
            zr = SA("zr", [128, 16, CH]); zi = SA("zi", [128, 16, CH])
            m1 = SA("m1", [128, 16, CH]); m2 = SA("m2", [128, 16, CH])
            Srb = [SA(f"Srb{i}", [128, 16, CH + 1], BF16) for i in range(2)]
            Sib = [SA(f"Sib{i}", [128, 16, CH + 1], BF16) for i in range(2)]
            car = SA("car", [128, 16]); cai = SA("cai", [128, 16])
            c1 = SA("c1", [128, 16]); c2_ = SA("c2", [128, 16]); c3 = SA("c3", [128, 16])
            zT = SA("zT", [128, 4, BLK], BF16)
            sgl = [SA(f"sgl{i}", [128, BLK]) for i in range(2)]

            def P1(blk):
                pb = blk % 2
                tok0 = blk * BLK
                hT_ = hT[pb]

                def proj_fm(ps, c0, M):
                    b.mm([dict(out=ps, lhsT=Win[:, k, c0:c0 + M], rhs=hT_[:, k, :], start=(k == 0), stop=(k == 7))
                          for k in range(8)])
                for i in range(NTI):
                    xt_ = xa[i % 2]
                    t.dma('sp', xt_[:], x[tok0 + 128 * i: tok0 + 128 * (i + 1), :], f'xa{i % 2}')
                    b.act(hb[:], xt_[:], AF.Square, accum_out=ss[:, i:i + 1])
                    b.act(rstd[:, i:i + 1], ss[:, i:i + 1], AF.Sqrt, scale=1.0 / 1024, bias=epsc[:, 0:1])
                    b.recip(rstd[:, i:i + 1], rstd[:, i:i + 1])
                    b.act(hb[:], xt_[:], AF.Copy, scale=rstd[:, i:i + 1])
                    yield
                    b.tr([(pT[:, 128 * k:128 * (k + 1)], hb[:, 128 * k:128 * (k + 1)]) for k in range(8)], identb[:])
                    b.cp(hT_[:, :, 128 * i:128 * (i + 1)], pT[:].rearrange("p (k c) -> p k c", c=128))
                    yield
                proj_fm(h1a[0:16, 0:BLK], 1024, 16)
                b.cp(gkl2[pb][0:16, :], h1a[0:16, 0:BLK])
                yield
                for tt_ in range(4):
                    ps = (h1a, h1b)[tt_ % 2]
                    proj_fm(ps[:, 0:BLK], 1552 + 128 * tt_, 128)
                    b.act(uT[pb][:, tt_, :], ps[:, 0:BLK], AF.Copy)
                    yield
                for h in range(4):
                    b.mm([dict(out=h1b[0:64, 0:BLK], lhsT=WG[:, 64 * h:64 * h + 64], rhs=gkl2[pb][:],
                               start=True, stop=True)])
                    b.act(la[:], h1b[0:64, 0:BLK], AF.Exp, scale=-1.0)
                    yield
                    b.act(la[:], la[:], AF.Ln, bias=1.0)
                    b.scan(cT[:], rmask[0:64, :], la[:])
                    b.act(eb[:], cT[:], AF.Exp, scale=-1.0 / 16)
                    b.act(enb[:], cT[:], AF.Exp, scale=1.0 / 16)
                    b.cp(dec[pb][:, h, :], eb[:, 127::128])
                    yield
                    proj_fm(h1a[0:64, 0:BLK], 64 * h, 64)
                    b.stt(qe[pb][:, h, :], h1a[0:64, 0:BLK], 0.125, eb[:], ALU.mult, ALU.mult)
                    yield
                    proj_fm(h1b[0:64, 0:BLK], 256 + 64 * h, 64)
                    b.tt(ke[pb][:, h, :], h1b[0:64, 0:BLK], enb[:], ALU.mult)
                    yield

            def P2(blk):
                pb = blk % 2
                hT_, qe_, ke_, gk_ = hT[pb], qe[pb], ke[pb], gkl2[pb]
                if blk % NB == 0:
                    b.memset(Sg[:], 0.0); b.memset(Sgb[:], 0.0)
                for i in range(NTI):
                    cs = slice(128 * i, 128 * (i + 1))

                    def proj_tm(ps, c0, N):
                        b.mm([dict(out=ps, lhsT=hT_[:, k, cs], rhs=Win[:, k, c0:c0 + N], start=(k == 0), stop=(k == 7))
                              for k in range(8)])
                    b.mm([dict(out=h4b[:, 0:256], lhsT=gk_[:, cs], rhs=WG[:], start=True, stop=True)])
                    b.act(latm[:], h4b[:, 0:256], AF.Exp, scale=-1.0)
                    yield
                    b.act(latm[:], latm[:], AF.Ln, bias=1.0)
                    b.mm([dict(out=h4b[:, 0:256], lhsT=umask[:], rhs=latm[:], start=True, stop=True)])
                    b.act(edt[:], h4b[:, 0:256], AF.Exp, scale=-1.0 / 16)
                    yield
                    proj_tm(h4a[:, 0:256], 256, 256)
                    b.tt(kd[:], h4a[:, 0:256], edt[:], ALU.mult)
                    yield
                    proj_tm(p3[:], 512, 512)
                    b.act(vtm[:], p3[:], AF.Copy)
                    yield
                    proj_tm(p3[:], 1040, 512)
                    b.act(sg[:], p3[:], AF.Silu)
                    b.tt(sg[:], sg[:], gnw[:], ALU.mult)
                    yield
                    b.mm([dict(out=pOb[:, 128 * h:128 * (h + 1)], lhsT=ke_[:, h, cs], rhs=qe_[:, h, cs],
                               start=True, stop=True) for h in range(4)])
                    b.tt(scT[:], pOb[:], cmask[:], ALU.mult)
                    yield
                    ops = []
                    for h in range(4):
                        hs = slice(128 * h, 128 * (h + 1))
                        ops.append(dict(out=pOa[:, hs], lhsT=scT[:, hs], rhs=vtm[:, hs], start=True, stop=False))
                        ops.append(dict(out=pOa[:, hs], lhsT=qe_[:, h, cs], rhs=Sgb[:, h, :], start=False, stop=True))
                    b.mm(ops)
                    upd = (h4a, h4b)
                    for hp in range(2):
                        b.mm([dict(out=upd[hp][0:64, 128 * (h % 2):128 * (h % 2 + 1)], lhsT=kd[:, 64 * h:64 * (h + 1)],
                                   rhs=vtm[:, 128 * h:128 * (h + 1)], start=True, stop=True)
                              for h in (2 * hp, 2 * hp + 1)])
                    for h in range(4):
                        b.stt(Sg[:, h, :], Sg[:, h, :], dec[pb][:, h, i:i + 1],
                              upd[h // 2][0:64, 128 * (h % 2):128 * (h % 2 + 1)], ALU.mult, ALU.add)
                    b.act(Sgb[:], Sg[:], AF.Copy)
                    for h in range(4):
                        b.act(sq2[:], pOa[:, 128 * h:128 * (h + 1)], AF.Square, accum_out=ss4[:, h:h + 1])
                    b.act(rs4[:], ss4[:], AF.Sqrt, scale=1.0 / 128, bias=epsc[:, 0:1])
                    b.recip(rs4[:], rs4[:])
                    for h in range(4):
                        hs = slice(128 * h, 128 * (h + 1))
                        b.stt(mixtm[:, hs], pOa[:, hs], rs4[:, h:h + 1], sg[:, hs], ALU.mult, ALU.mult)
                    yield
                    b.tr([(pT[:, 128 * h:128 * (h + 1)], mixtm[:, 128 * h:128 * (h + 1)]) for h in range(4)], identb[:])
                    b.cp(mixG[pb][:, :, cs], pT[:, 0:512].rearrange("p (k c) -> p k c", c=128))
                    yield

            def S5(blk):
                pb = blk % 2
                u_ = uT[pb]
                if blk % NB == 0:
                    b.memset(car[:], 0.0); b.memset(cai[:], 0.0)
                    b.memset(Srb[(blk + 1) % 2][:], 0.0); b.memset(Sib[(blk + 1) % 2][:], 0.0)
                zbanks = {0: pH5[:, 0:8 * CH], 1: pH2[:, 0:8 * CH]}
                Z5 = Zsb[:].rearrange("p (t j) r c -> p t j r c", j=4)
                for th_ in range(2):
                    for H in range(2):
                        zb = zbanks[H].rearrange("p (t j r c) -> p t j r c", t=2, j=2, r=2)
                        ops = []
                        for tl in range(2):
                            tt_ = 2 * th_ + tl
                            uv = u_[64 * H:64 * H + 64, tt_, :].rearrange("p (c i) -> p c i", i=L1)
                            for jj in range(2):
                                for ri in range(2):
                                    for js in range(L1):
                                        ops.append(dict(out=zb[:, tl, jj, ri, :],
                                                        lhsT=ZW[64 * H:64 * H + 64, tt_, js, jj, ri, :],
                                                        rhs=uv[:, :, js], start=(js == 0), stop=(js == L1 - 1)))
                        b.mm(ops)
                    for H in range(2):
                        zb = zbanks[H].rearrange("p (t j r c) -> p t j r c", t=2, j=2, r=2)
                        for tl in range(2):
                            b.act(Z5[:, 2 * th_ + tl, 2 * H:2 * H + 2, :, :], zb[:, tl, :, :, :], AF.Copy)
                    yield
                Zr, Zi = Zsb[:, :, 0, :], Zsb[:, :, 1, :]
                b.tt(c1[:], L8r[:], car[:], ALU.mult)
                b.tt(c2_[:], L8i[:], cai[:], ALU.mult)
                b.tt(c1[:], c1[:], c2_[:], ALU.subtract)
                b.tt(c2_[:], L8r[:], cai[:], ALU.mult)
                b.tt(c3[:], L8i[:], car[:], ALU.mult)
                b.tt(c2_[:], c2_[:], c3[:], ALU.add)
                b.tt(Zsb[:, :, 0, 0], Zsb[:, :, 0, 0], c1[:], ALU.add)
                b.tt(Zsb[:, :, 1, 0], Zsb[:, :, 1, 0], c2_[:], ALU.add)
                yield
                b.tt(m1[:], Cc[:], Zr, ALU.mult)
                b.tt(m2[:], Sn[:], Zi, ALU.mult)
                b.tt(zr[:], m1[:], m2[:], ALU.add)
                yield
                b.tt(m1[:], Cc[:], Zi, ALU.mult)
                b.tt(m2[:], Sn[:], Zr, ALU.mult)
                b.tt(zi[:], m1[:], m2[:], ALU.subtract)
                yield
                fl = lambda a_: a_[:].rearrange("p q c -> p (q c)")
                b.scan(fl(m1), fl(R8m), fl(zr))
                yield
                b.scan(fl(m2), fl(R8m), fl(zi))
                yield
                zsr, zsi = m1, m2
                So, Io = Srb[pb], Sib[pb]
                Sp, Ip = Srb[(blk + 1) % 2], Sib[(blk + 1) % 2]
                b.cp(So[:, :, 0:1], Sp[:, :, CH:CH + 1])
                b.cp(Io[:, :, 0:1], Ip[:, :, CH:CH + 1])
                b.tt(zr[:], Cc[:], zsr[:], ALU.mult)
                b.tt(zi[:], Sn[:], zsi[:], ALU.mult)
                b.tt(So[:, :, 1:CH + 1], zr[:], zi[:], ALU.subtract)
                yield
                b.tt(zr[:], Sn[:], zsr[:], ALU.mult)
                b.tt(zi[:], Cc[:], zsi[:], ALU.mult)
                b.tt(Io[:, :, 1:CH + 1], zr[:], zi[:], ALU.add)
                yield
                b.tt(c1[:], Cc[:, :, CH - 1], zsr[:, :, CH - 1], ALU.mult)
                b.tt(c2_[:], Sn[:, :, CH - 1], zsi[:, :, CH - 1], ALU.mult)
                b.tt(car[:], c1[:], c2_[:], ALU.subtract)
                b.tt(c1[:], Sn[:, :, CH - 1], zsr[:, :, CH - 1], ALU.mult)
                b.tt(c2_[:], Cc[:, :, CH - 1], zsi[:, :, CH - 1], ALU.mult)
                b.tt(cai[:], c1[:], c2_[:], ALU.add)
                yield
                for tt_ in range(4):
                    ps = (h5a, h5b)[tt_ % 2]
                    yv = ps[:, 0:BLK].rearrange("p (c i) -> p c i", i=L1)
                    uv = u_[:, tt_, :].rearrange("p (c i) -> p c i", i=L1)
                    ops = [dict(out=yv[:, :, d:L1], lhsT=Kbd[:, d, tt_, :], rhs=uv[:, :, 0:L1 - d],
                                start=(d == 0), stop=False) for d in range(L1)]
                    for jp in range(4):
                        q = 4 * tt_ + jp
                        for i in range(L1):
                            ops.append(dict(out=yv[32 * jp:32 * jp + 32, :, i], lhsT=OgR[:, i, q, :],
                                            rhs=So[:, q, 0:CH], start=False, stop=False, tp=(0, 32 * jp)))
                            ops.append(dict(out=yv[32 * jp:32 * jp + 32, :, i], lhsT=OgI[:, i, q, :],
                                            rhs=Io[:, q, 0:CH], start=False, stop=(jp == 3 and i == L1 - 1),
                                            tp=(0, 32 * jp)))
                    b.mm(ops)
                    b.act(zT[:, tt_, :], ps[:, 0:BLK], AF.Gelu_apprx_tanh)
                    yield
                for mt in range(4):
                    ps = (h2a, h2b)[mt % 2]
                    b.mm([dict(out=ps[:, 0:BLK], lhsT=Wglu[:, k, 128 * mt:128 * (mt + 1)], rhs=zT[:, k, :],
                               start=(k == 0), stop=(k == 3)) for k in range(4)])
                    b.act(sgl[mt % 2][:], ps[:, 0:BLK], AF.Sigmoid, bias=bglu[:, mt:mt + 1])
                    b.tt(mixS[pb][:, mt, :], zT[:, mt, :], sgl[mt % 2][:], ALU.mult)
                    yield

            def Rout(blk):
                pb = blk % 2
                tok0 = blk * BLK
                for i in range(NTI):
                    cs = slice(128 * i, 128 * (i + 1))
                    t.dma('sp', xr[:], x[tok0 + 128 * i: tok0 + 128 * (i + 1), :], 'xr')
                    for hf, po in ((0, pOa), (1, pOb)):
                        b.mm([dict(out=po[:], lhsT=(mixG[pb] if k < 4 else mixS[pb])[:, k % 4, cs],
                                   rhs=Wout[:, k, 512 * hf:512 * (hf + 1)], start=(k == 0), stop=(k == 7))
                              for k in range(8)])
                        b.tt(xr[:, 512 * hf:512 * (hf + 1)], xr[:, 512 * hf:512 * (hf + 1)], po[:], ALU.add)
                        yield
                    t.dma('pool', x1[tok0 + 128 * i: tok0 + 128 * (i + 1), :], xr[:], 'x1st')
                    yield

            def run_threads(gens_w):
                live = [[g, w] for g, w in gens_w if g is not None]
                while live:
                    for ent in list(live):
                        for _ in range(ent[1]):
                            try:
                                next(ent[0])
                            except StopIteration:
                                live.remove(ent)
                                break

            nblk_a = 0 if _DBG.get('skipA') else NBLK
            for k in range(nblk_a + 2 if nblk_a else 0):
                W = _DBG.get('w', (1, 1, 1, 1))
                order = _DBG.get('order', (0, 1, 2, 3))
                th = [
                    (P1(k) if k < nblk_a else None, W[0]),
                    (P2(k - 1) if 1 <= k <= nblk_a else None, W[1]),
                    (S5(k - 1) if 1 <= k <= nblk_a else None, W[2]),
                    (Rout(k - 2) if 2 <= k <= nblk_a + 1 else None, W[3]),
                ]
                run_threads([th[i] for i in order])
            t.barrier()

        esB = contextlib.ExitStack()
        with esB:
            def SBb(name, shape, dt=F32):
                return SB(name, shape, dt, stack=esB)
            BB = 512
            nw2 = SBb("nw2", [128, 8])
            t.dma('sp', nw2[:], norm_mlp_w.rearrange("(k p) -> p k", p=128), 'nw2', allow_slow_non_contiguous=True)
            Wup = SBb("Wup", [128, 8, 4096], BF16)
            Wdn = SBb("Wdn", [128, 32, 1024], BF16)
            nfw = SBb("nfw", [128, 1024])
            t.dma('sp', nfw[:], norm_final_w.partition_broadcast(128), 'nfw')
            esT = contextlib.ExitStack()
            with esT:
                stg2 = [SB("sg0", [128, 2048], F32, stack=esT), SB("sg1", [128, 2048], F32, stack=esT)]
                for k2 in range(16):
                    k, hf = k2 // 2, k2 % 2
                    t.dma('sp', stg2[k2 % 2][:], w_up[128 * k:128 * (k + 1), 2048 * hf:2048 * (hf + 1)], f'sg{k2 % 2}')
                    if k2 % 2 == 0:
                        b.act(Wup[:, k, 2048 * hf:2048 * (hf + 1)], stg2[0][:], AF.Copy, scale=nw2[:, k:k + 1])
                    else:
                        b.ts(Wup[:, k, 2048 * hf:2048 * (hf + 1)], stg2[1][:], nw2[:, k:k + 1], ALU.mult)
                for k2 in range(16):
                    t.dma('sp', stg2[k2 % 2][:].rearrange("p (a c) -> p a c", a=2),
                          w_down[256 * k2:256 * (k2 + 1), :].rearrange("(a p) c -> p a c", p=128), f'sg{k2 % 2}')
                    src = stg2[k2 % 2][:].rearrange("p (a c) -> p a c", a=2)
                    if k2 % 2 == 0:
                        b.act(Wdn[:, 2 * k2:2 * k2 + 2, :], src, AF.Copy)
                    else:
                        b.cp(Wdn[:, 2 * k2:2 * k2 + 2, :], src)
                t.barrier()
            ya = [SBb(f"ya{i}", [128, 1024]) for i in range(2)]
            yr = [SBb(f"yr{i}", [128, 1024]) for i in range(2)]
            ssb = SBb("ssb", [128, 4]); rsb = SBb("rsb", [128, 4])
            hbb = SBb("hbb", [128, 1024], BF16)
            h2T = [SBb(f"h2T{i}", [128, 8, BB], BF16) for i in range(1)]
            rl = [SBb(f"rl{i}", [128, BB]) for i in range(2)]
            aT = SBb("aT", [128, 32, BB], BF16)
            ss2 = SBb("ss2", [128, 2]); rs2 = SBb("rs2", [128, 2])
            NB2 = NT // BB
            pups = [pOa, pOb, p3]
            for blk in range(0 if _DBG.get('skipB') else NB2):
                tok0 = blk * BB
                for i in range(4):
                    y_ = ya[i % 2]
                    t.dma('sp', y_[:], x1[tok0 + 128 * i: tok0 + 128 * (i + 1), :], f'ya{i % 2}')
                    b.act(hbb[:], y_[:], AF.Square, accum_out=ssb[:, i:i + 1])
                    b.act(rsb[:, i:i + 1], ssb[:, i:i + 1], AF.Sqrt, scale=1.0 / 1024, bias=epsc[:, 0:1])
                    b.recip(rsb[:, i:i + 1], rsb[:, i:i + 1])
                    b.act(hbb[:], y_[:], AF.Copy, scale=rsb[:, i:i + 1])
                    b.tr([(pT[:, 128 * k:128 * (k + 1)], hbb[:, 128 * k:128 * (k + 1)]) for k in range(8)], identb[:])
                    b.cp(h2T[0][:, :, 128 * i:128 * (i + 1)], pT[:].rearrange("p (k c) -> p k c", c=128))
                for m in range(32):
                    ps = pups[m % 3]
                    b.mm([dict(out=ps[:], lhsT=Wup[:, k, 128 * m:128 * (m + 1)], rhs=h2T[0][:, k, :],
                               start=(k == 0), stop=(k == 7)) for k in range(8)])
                    b.act(rl[m % 2][:], ps[:], AF.Relu)
                    b.tt(aT[:, m, :], rl[m % 2][:], rl[m % 2][:], ALU.mult, e=('dve' if m % 4 != 3 else 'pool'))
                for i in range(4):
                    cs = slice(128 * i, 128 * (i + 1))
                    y_ = yr[i % 2]
                    t.dma('sp', y_[:], x1[tok0 + 128 * i: tok0 + 128 * (i + 1), :], f'yr{i % 2}')
                    pd = ((pH1, pH2), (pH4, pH5))[i % 2]
                    for hf in range(2):
                        po = pd[hf]
                        b.mm([dict(out=po[:], lhsT=aT[:, k, cs], rhs=Wdn[:, k, 512 * hf:512 * (hf + 1)],
                                   start=(k == 0), stop=(k == 31)) for k in range(32)])
                        b.tt(y_[:, 512 * hf:512 * (hf + 1)], y_[:, 512 * hf:512 * (hf + 1)], po[:], ALU.add)
                    b.act(hbb[:], y_[:], AF.Square, accum_out=ss2[:, i % 2:i % 2 + 1])
                    b.act(rs2[:, i % 2:i % 2 + 1], ss2[:, i % 2:i % 2 + 1], AF.Sqrt, scale=1.0 / 1024, bias=epsc[:, 0:1])
                    b.recip(rs2[:, i % 2:i % 2 + 1], rs2[:, i % 2:i % 2 + 1])
                    b.stt(y_[:], y_[:], rs2[:, i % 2:i % 2 + 1], nfw[:], ALU.mult, ALU.mult)
                    t.dma('pool', out[tok0 + 128 * i: tok0 + 128 * (i + 1), :], y_[:], f'ost{i % 2}')
            t.barrier()
    return nc


def _consts():
    j = np.arange(128)[:, None]
    i = np.arange(128)[None, :]
    cm = (j <= i).astype(np.float32)
    um = (j > i).astype(np.float32)
    rm = np.ones((128, BLK_A), np.float32)
    rm[:, ::128] = 0.0
    rs = np.ones((128, 16, BLK_A // L1_A), np.float32)
    rs[:, :, 0] = 0.0
    return {
        "c_ident": np.eye(128, dtype=np.float32),
        "c_cmask": np.tile(cm, (1, 4)),
        "c_umask": um,
        "c_rmask": rm,
        "c_reset": rs.reshape(128, 16 * (BLK_A // L1_A)),
    }


_CACHE = {}
_DBG = {}


def kernel(**inputs):
    x = np.ascontiguousarray(inputs["x"], dtype=np.float32)
    Bn, S, Dm = x.shape
    ncores = NCORES if Bn % NCORES == 0 else 1
    nseq = Bn // ncores
    key = (nseq, S)
    if key not in _CACHE:
        _CACHE[key] = build(nseq, S)
    nc = _CACHE[key]
    shared = dict(_consts())
    for k, v in inputs.items():
        if k == "x":
            continue
        a = np.ascontiguousarray(v, dtype=np.float32)
        if k != "norm_final_w":
            a = a[0]
        shared[k] = np.ascontiguousarray(a)
    in_maps = []
    for c in range(ncores):
        m = dict(shared)
        m["x"] = np.ascontiguousarray(x[c * nseq:(c + 1) * nseq].reshape(nseq * S, Dm))
        in_maps.append(m)
    res = run_bass_kernel_spmd(nc, in_maps, core_ids=list(range(ncores)))
    outs = [np.asarray(r["out"]).reshape(nseq, S, Dm) for r in res.results]
    return np.concatenate(outs, axis=0).astype(np.float32)
```
